# Optimizing a Trainium2 kernel written in Bass

```python
import jax, jax.numpy as jnp
from jax import lax
import numpy as np

D_MODEL = 2048
BATCH = 4
SEQ = 4096
DEPTH = 4

GRID_W = 64
CTX_LEN = 256
N_MIXERS = 2
N_RET_LAYERS = (DEPTH + 1) // 2
N_NA_LAYERS = DEPTH // 2
N_MOD = 9
D_FF = ((8 * D_MODEL // 3 + 127) // 128) * 128
RET_HEADS = 8
RET_DK = D_MODEL // RET_HEADS
RET_DV = 2 * RET_DK
RET_V_WIDTH = RET_HEADS * RET_DV
RET_CHUNK = 128
NA_HEADS = 16
NA_DH = D_MODEL // NA_HEADS
NA_KR = 8
NA_KC = 16
ROPE_BASE = 10000.0
EPS = 1e-6

kernel_name = 'hybrid_retention_natten_macaron_dit'


def _rmsnorm(x, g):
    xf = x.astype(jnp.float32)
    y = xf * lax.rsqrt(jnp.mean(xf * xf, axis=-1, keepdims=True) + EPS)
    return (y * g.astype(jnp.float32)).astype(x.dtype)


def _ada_norm(x, g, shift, scale):
    return _rmsnorm(x, g) * (1 + scale) + shift


def _swiglu(h, w_gu, w_down):
    gate, up = jnp.split(h @ w_gu, 2, axis=-1)
    return (jax.nn.silu(gate) * up) @ w_down


def _heads(t, n_heads):
    b, n, w = t.shape
    return t.reshape(b, n, n_heads, w // n_heads).transpose(0, 2, 1, 3)


def _axial_rope_angles(n_tok, dim):
    t = jnp.arange(n_tok)
    row = (t // GRID_W).astype(jnp.float32)
    col = (t % GRID_W).astype(jnp.float32)
    half = dim // 2
    freqs = ROPE_BASE ** (-jnp.arange(0, half, 2, dtype=jnp.float32) / half)
    return row[:, None] * freqs, col[:, None] * freqs


def _rope_half(x, ang):
    x1, x2 = jnp.split(x, 2, axis=-1)
    cos, sin = jnp.cos(ang), jnp.sin(ang)
    return jnp.concatenate([x1 * cos - x2 * sin, x1 * sin + x2 * cos], axis=-1)


def _apply_axial_rope(x, ang_r, ang_c):
    xr, xc = jnp.split(x, 2, axis=-1)
    return jnp.concatenate([_rope_half(xr, ang_r), _rope_half(xc, ang_c)], axis=-1)


def _retention_chunks(q, k, v, log_gamma, state0):
    b, h, n, _ = q.shape
    dv = v.shape[-1]
    nc = n // RET_CHUNK
    pos = jnp.arange(RET_CHUNK, dtype=jnp.float32)
    lg = log_gamma[:, None]
    diff = pos[:, None] - pos[None, :]
    intra = jnp.where(diff[None] >= 0, jnp.exp(jnp.maximum(diff, 0.0)[None] * lg[:, :, None]), 0.0)
    q_decay = jnp.exp((pos + 1.0) * lg)[None, :, :, None]
    k_decay = jnp.exp((RET_CHUNK - 1.0 - pos) * lg)[None, :, :, None]
    c_decay = jnp.exp(RET_CHUNK * lg)[None, :, :, None]

    def to_chunks(t):
        return t.reshape(b, h, nc, RET_CHUNK, t.shape[-1]).transpose(2, 0, 1, 3, 4)

    def step(state, inp):
        qc, kc, vc = inp
        s = jnp.einsum('bhid,bhjd->bhij', qc, kc) * intra[None]
        o = jnp.einsum('bhij,bhjv->bhiv', s, vc) + jnp.einsum('bhid,bhdv->bhiv', qc, state) * q_decay
        state = state * c_decay + jnp.einsum('bhjd,bhjv->bhdv', kc * k_decay, vc)
        return state, o

    _, outs = lax.scan(step, state0, (to_chunks(q), to_chunks(k), to_chunks(v)))
    return outs.transpose(1, 2, 0, 3, 4).reshape(b, h, n, dv)


def _context_state(k, v, log_gamma, reverse):
    l = k.shape[2]
    pos = jnp.arange(l, dtype=jnp.float32)
    expo = pos if reverse else (l - 1.0 - pos)
    w = jnp.exp(expo[None, :] * log_gamma[:, None])
    return jnp.einsum('bhld,bhlv->bhdv', k * w[None, :, :, None], v)


def _retention_out(o, g, w_out, dtype):
    mu = jnp.mean(o, axis=-1, keepdims=True)
    var = jnp.mean(jnp.square(o - mu), axis=-1, keepdims=True)
    on = (o - mu) * lax.rsqrt(var + EPS)
    b, h, n, dv = on.shape
    on = on.transpose(0, 2, 1, 3).reshape(b, n, h * dv).astype(dtype)
    return (on * jax.nn.silu(g)) @ w_out


def _retention_mixer(h_lat, h_ctx, w_in, w_out, log_decay, ang_r, ang_c, need_ctx):
    lg_f = -jnp.exp(log_decay[0].astype(jnp.float32))
    lg_b = -jnp.exp(log_decay[1].astype(jnp.float32))

    def project(h):
        q, k, v, g = jnp.split(h @ w_in, [D_MODEL, 2 * D_MODEL, 2 * D_MODEL + RET_V_WIDTH], axis=-1)
        q = _heads(q, RET_HEADS).astype(jnp.float32)
        k = _heads(k, RET_HEADS).astype(jnp.float32) * (RET_DK ** -0.5)
        v = _heads(v, RET_HEADS).astype(jnp.float32)
        return q, k, v, g

    flip = lambda t: jnp.flip(t, axis=2)
    q, k, v, g = project(h_lat)
    q = _apply_axial_rope(q, ang_r, ang_c)
    k = _apply_axial_rope(k, ang_r, ang_c)
    qc, kc, vc, gc = project(h_ctx)
    state_f = _context_state(kc, vc, lg_f, False)
    state_b = _context_state(kc, vc, lg_b, True)
    o = _retention_chunks(q, k, v, lg_f, state_f) + flip(_retention_chunks(flip(q), flip(k), flip(v), lg_b, state_b))
    y_lat = _retention_out(o, g, w_out, h_lat.dtype)
    y_ctx = None
    if need_ctx:
        zero = jnp.zeros_like(state_f)
        oc = _retention_chunks(qc, kc, vc, lg_f, zero) + flip(_retention_chunks(flip(qc), flip(kc), flip(vc), lg_b, zero))
        y_ctx = _retention_out(oc, gc, w_out, h_ctx.dtype)
    return y_lat, y_ctx


def _na_mixer(h_lat, h_ctx, w_in, w_out, rpb, need_ctx):
    b, n, _ = h_lat.shape
    rows = n // GRID_W
    kr = min(NA_KR, rows)
    q, k, v = jnp.split(h_lat @ w_in, 3, axis=-1)
    q = _heads(q, NA_HEADS) * (NA_DH ** -0.5)
    k = _heads(k, NA_HEADS)
    v = _heads(v, NA_HEADS)
    qc, kc, vc = jnp.split(h_ctx @ w_in, 3, axis=-1)
    qc = _heads(qc, NA_HEADS) * (NA_DH ** -0.5)
    kc = _heads(kc, NA_HEADS)
    vc = _heads(vc, NA_HEADS)
    grid = lambda t: t.reshape(b, NA_HEADS, rows, GRID_W, NA_DH)
    q_g, k_g, v_g = grid(q), grid(k), grid(v)
    col = jnp.arange(GRID_W)
    cs = jnp.clip(col - NA_KC // 2, 0, GRID_W - NA_KC)
    col_idx = cs[:, None] + jnp.arange(NA_KC)[None, :]
    dc = col_idx - col[:, None] + (NA_KC - 1)

    def row_block(r):
        rs = jnp.clip(r - kr // 2, 0, rows - kr)
        qb = lax.dynamic_index_in_dim(q_g, r, axis=2, keepdims=False)
        kb = lax.dynamic_slice_in_dim(k_g, rs, kr, axis=2)
        vb = lax.dynamic_slice_in_dim(v_g, rs, kr, axis=2)
        kw = kb[:, :, :, col_idx]
        vw = vb[:, :, :, col_idx]
        dr = rs + jnp.arange(kr) - r + (NA_KR - 1)
        bias = rpb[:, dr[:, None, None], dc[None, :, :]].transpose(0, 2, 1, 3)
        s_loc = jnp.einsum('bhqd,bhrqkd->bhqrk', qb, kw).astype(jnp.float32) + bias[None].astype(jnp.float32)
        s_ctx = jnp.einsum('bhqd,bhcd->bhqc', qb, kc).astype(jnp.float32)
        s = jnp.concatenate([s_loc.reshape(b, NA_HEADS, GRID_W, kr * NA_KC), s_ctx], axis=-1)
        p = jax.nn.softmax(s, axis=-1)
        p_loc = p[..., :kr * NA_KC].reshape(b, NA_HEADS, GRID_W, kr, NA_KC)
        p_ctx = p[..., kr * NA_KC:]
        o = jnp.einsum('bhqrk,bhrqkd->bhqd', p_loc, vw) + jnp.einsum('bhqc,bhcd->bhqd', p_ctx, vc)
        return o.astype(h_lat.dtype)

    o = lax.map(row_block, jnp.arange(rows))
    o = o.transpose(1, 0, 3, 2, 4).reshape(b, n, NA_HEADS * NA_DH)
    y_lat = o @ w_out
    y_ctx = None
    if need_ctx:
        pc = jax.nn.softmax(jnp.einsum('bhqd,bhcd->bhqc', qc, kc).astype(jnp.float32), axis=-1)
        oc = jnp.einsum('bhqc,bhcd->bhqd', pc, vc).astype(h_ctx.dtype)
        bc, _, lc, _ = oc.shape
        y_ctx = oc.transpose(0, 2, 1, 3).reshape(bc, lc, NA_HEADS * NA_DH) @ w_out
    return y_lat, y_ctx


def setup_inputs(seed: int = 0) -> dict:
    key = jax.random.key(seed)
    ks = jax.random.split(key, 16)
    f32 = jnp.float32
    nrm = lambda k, shape, s: jax.random.normal(k, shape, f32) * s
    base_decay = jnp.log(-jnp.log1p(-(2.0 ** (-5.0 - jnp.arange(RET_HEADS, dtype=f32)))))
    return {
        'x': nrm(ks[0], (BATCH, SEQ, D_MODEL), 1.0),
        'c': nrm(ks[1], (BATCH, D_MODEL), 1.0),
        'ctx': nrm(ks[2], (BATCH, CTX_LEN, D_MODEL), 1.0),
        'c_ctx': nrm(ks[3], (D_MODEL,), 1.0),
        'ada_w': nrm(ks[4], (DEPTH, D_MODEL, N_MOD * D_MODEL), 0.5 * D_MODEL ** -0.5),
        'ada_b': nrm(ks[5], (DEPTH, N_MOD * D_MODEL), 0.01),
        'norm_g': 1.0 + nrm(ks[6], (DEPTH, 3, D_MODEL), 0.02),
        'ffn_w_gu': nrm(ks[7], (DEPTH, 2, D_MODEL, 2 * D_FF), D_MODEL ** -0.5),
        'ffn_w_down': nrm(ks[8], (DEPTH, 2, D_FF, D_MODEL), D_FF ** -0.5),
        'ret_w_in': nrm(ks[9], (N_RET_LAYERS, D_MODEL, 2 * D_MODEL + 2 * RET_V_WIDTH), D_MODEL ** -0.5),
        'ret_w_out': nrm(ks[10], (N_RET_LAYERS, RET_V_WIDTH, D_MODEL), RET_V_WIDTH ** -0.5),
        'ret_log_decay': base_decay[None, None, :] + nrm(ks[11], (N_RET_LAYERS, 2, RET_HEADS), 0.05),
        'na_w_in': nrm(ks[12], (N_NA_LAYERS, D_MODEL, 3 * D_MODEL), D_MODEL ** -0.5),
        'na_w_out': nrm(ks[13], (N_NA_LAYERS, D_MODEL, D_MODEL), D_MODEL ** -0.5),
        'na_rpb': nrm(ks[14], (N_NA_LAYERS, NA_HEADS, 2 * NA_KR - 1, 2 * NA_KC - 1), 0.1),
        'final_g': 1.0 + nrm(ks[15], (D_MODEL,), 0.02),
    }


def reference(x, c, ctx, c_ctx, ada_w, ada_b, norm_g, ffn_w_gu, ffn_w_down, ret_w_in, ret_w_out,
              ret_log_decay, na_w_in, na_w_out, na_rpb, final_g):
    n_lat = x.shape[1]
    ang_r, ang_c = _axial_rope_angles(n_lat, RET_DK)
    silu_c = jax.nn.silu(c)
    silu_cc = jax.nn.silu(c_ctx)
    for i in range(DEPTH):
        last = i == DEPTH - 1
        mod = (silu_c @ ada_w[i] + ada_b[i]).reshape(c.shape[0], N_MOD, D_MODEL)
        m = [mod[:, j][:, None, :] for j in range(N_MOD)]
        mod_c = (silu_cc @ ada_w[i] + ada_b[i]).reshape(N_MOD, D_MODEL)
        mc = [mod_c[j][None, None, :] for j in range(N_MOD)]
        x = x + 0.5 * m[2] * _swiglu(_ada_norm(x, norm_g[i, 0], m[0], m[1]), ffn_w_gu[i, 0], ffn_w_down[i, 0])
        ctx = ctx + 0.5 * mc[2] * _swiglu(_ada_norm(ctx, norm_g[i, 0], mc[0], mc[1]), ffn_w_gu[i, 0], ffn_w_down[i, 0])
        hx = _ada_norm(x, norm_g[i, 1], m[3], m[4])
        hc = _ada_norm(ctx, norm_g[i, 1], mc[3], mc[4])
        j = i // N_MIXERS
        if i % N_MIXERS == 0:
            yx, yc = _retention_mixer(hx, hc, ret_w_in[j], ret_w_out[j], ret_log_decay[j], ang_r, ang_c, not last)
        else:
            yx, yc = _na_mixer(hx, hc, na_w_in[j], na_w_out[j], na_rpb[j], not last)
        x = x + m[5] * yx
        x = x + 0.5 * m[8] * _swiglu(_ada_norm(x, norm_g[i, 2], m[6], m[7]), ffn_w_gu[i, 1], ffn_w_down[i, 1])
        if not last:
            ctx = ctx + mc[5] * yc
            ctx = ctx + 0.5 * mc[8] * _swiglu(_ada_norm(ctx, norm_g[i, 2], mc[6], mc[7]), ffn_w_gu[i, 1], ffn_w_down[i, 1])
    return _rmsnorm(x, final_g)
```

```python
import contextlib
import numpy as np
import concourse.bass as bass
import concourse.mybir as mybir
from concourse.bass_utils import run_bass_kernel_spmd

F32 = mybir.dt.float32
BF16 = mybir.dt.bfloat16
AF = mybir.ActivationFunctionType
ALU = mybir.AluOpType
AX = mybir.AxisListType

D = 2048
NCH = 16
DFF = 5504
NF = 43
EPS = 1e-6
GRID_W = 64
CTX = 256
RH, RDK, RDV = 8, 256, 512
NAH, NADH = 16, 128
NA_KR, NA_KC = 8, 16
ENGS = ["pe", "act", "dve", "pool", "sp"]
SELF_SYNC = True


class Res:
    __slots__ = ("w", "rd")

    def __init__(self):
        self.w = None
        self.rd = []


class Op:
    __slots__ = ("eng", "fn", "deps", "dma", "tok", "need_tok", "bg", "idx")

    def __init__(self, eng, fn, dma, bg):
        self.eng = eng
        self.fn = fn
        self.dma = dma
        self.bg = bg
        self.deps = []
        self.tok = None
        self.need_tok = False


class Prog:
    def __init__(self, nc):
        self.nc = nc
        self.ops = []
        self.streams = {e: [] for e in ENGS}
        self.ndma_sems = {"sp": 24, "act": 4, "pool": 12}
        self.dma_hist = {e: [] for e in ENGS}
        self.last = {e: None for e in ENGS}
        self.since_fence = []
        self.fence_deps = {e: [] for e in ENGS}

    def op(self, eng, fn, reads=(), writes=(), dma=False, bg=False):
        o = Op(eng, fn, dma, bg)
        o.idx = len(self.ops)
        deps = {}

        def add(d):
            key = id(d) if d.dma else d.eng
            p = deps.get(key)
            if p is None or p.idx < d.idx:
                deps[key] = d

        for r in reads:
            if r.w is not None:
                add(r.w)
        for w in writes:
            if w.w is not None:
                add(w.w)
            for x in w.rd:
                add(x)
        if dma:
            h = self.dma_hist[eng]
            k = self.ndma_sems[eng]
            if len(h) >= k:
                add(h[len(h) - k])
            h.append(o)
            if not bg:
                self.since_fence.append(o)
        if self.fence_deps[eng] and not bg:
            for p in self.fence_deps[eng]:
                add(p)
            self.fence_deps[eng] = []
        for d in deps.values():
            if (not d.dma) and (not dma) and d.eng == eng and (eng == "pe" or not SELF_SYNC):
                continue
            d.need_tok = True
            o.deps.append(d)
        for w in writes:
            w.w = o
            w.rd = []
        for r in reads:
            if dma:
                r.rd.append(o)
            else:
                r.rd = [x for x in r.rd if x.dma or x.eng != eng]
                r.rd.append(o)
        self.ops.append(o)
        self.streams[eng].append(o)
        if not dma and not bg:
            self.last[eng] = o
        return o

    def fence(self):
        deps = [o for o in self.last.values() if o is not None]
        deps += [o for o in self.since_fence if not o.need_tok]
        self.since_fence = []
        for e in ENGS:
            self.fence_deps[e] = list(deps)

    def emit(self, final_waits=()):
        nc = self.nc
        for o in final_waits:
            o.need_tok = True
        with contextlib.ExitStack() as st:
            esem = {e: st.enter_context(nc.semaphore("s_" + e)) for e in ["pe", "act", "dve", "pool"]}
            dsem = {e: [st.enter_context(nc.semaphore("d_%s%d" % (e, i))) for i in range(n)]
                    for e, n in self.ndma_sems.items()}
            cnt = {e: 0 for e in esem}
            dcnt = {e: [0] * n for e, n in self.ndma_sems.items()}
            dk = {e: 0 for e in self.ndma_sems}
            for o in self.ops:
                if o.dma:
                    e = o.eng
                    k = dk[e] % self.ndma_sems[e]
                    dk[e] += 1
                    dcnt[e][k] += 16
                    o.tok = (dsem[e][k], dcnt[e][k])
                elif o.need_tok:
                    cnt[o.eng] += 1
                    o.tok = (esem[o.eng], cnt[o.eng])
            block = st.enter_context(nc.Block())

            def run(ename):
                def body(eng):
                    known = {}
                    for o in self.streams[ename]:
                        for d in o.deps:
                            sem, val = d.tok
                            if known.get(id(sem), 0) >= val:
                                continue
                            known[id(sem)] = val
                            eng.wait_ge(sem, val)
                        ins = o.fn(eng)
                        if o.tok is not None:
                            ins.then_inc(o.tok[0], 16 if o.dma else 1)
                    if ename == "sp":
                        for o in final_waits:
                            sem, val = o.tok
                            if known.get(id(sem), 0) >= val:
                                continue
                            known[id(sem)] = val
                            eng.wait_ge(sem, val)
                return body

            block.tensor(run("pe"))
            block.scalar(run("act"))
            block.vector(run("dve"))
            block.gpsimd(run("pool"))
            block.sync(run("sp"))


class Arena:
    def __init__(self, nc, words):
        self.t = nc.alloc_sbuf_tensor("arena", [128, words], F32)
        self.words = words
        self.off = 0

    def alloc(self, shape, dtype):
        n = int(np.prod(shape))
        w = n if dtype == F32 else (n + 1) // 2
        w = (w + 7) // 8 * 8
        assert self.off + w <= self.words, ("arena overflow", self.off, w, self.words)
        v = self.t[:, self.off:self.off + w]
        self.off += w
        if dtype != F32:
            v = v.bitcast(dtype)
        v = v[:, :n]
        if len(shape) == 2:
            v = v.rearrange("p (a b) -> p a b", b=shape[1])
        elif len(shape) == 3:
            v = v.rearrange("p (a b c) -> p a b c", b=shape[1], c=shape[2])
        return v

    def mark(self):
        return self.off

    def reset(self, m):
        self.off = m


class K:
    def __init__(self, nlat=4096, depth=4):
        self.nlat = nlat
        self.depth = depth
        self.rows = nlat // GRID_W
        self.nlt = nlat // 128
        self.nt = self.nlt + 2
        self.ntok = nlat + CTX
        self.blocks = [(i * 512, 512) for i in range(nlat // 512)] + [(nlat, CTX)]
        self.mixers = ["ret" if i % 2 == 0 else "na" for i in range(depth)]
        self.nret = (depth + 1) // 2
        self.nna = depth // 2

    def build(self):
        nc = bass.Bass("TRN2", target_bir_lowering=False)
        self.nc = nc
        P = self.P = Prog(nc)
        dep, ntok = self.depth, self.ntok
        ein = lambda n, s, dt=F32: nc.dram_tensor(n, s, dt, kind="ExternalInput").ap()
        scr = lambda n, s, dt=BF16: nc.dram_tensor(n, s, dt).ap()
        self.xin = ein("xin", [D, ntok])
        self.out = nc.dram_tensor("out", [D, self.nlat], F32, kind="ExternalOutput").ap()
        self.cc_in = ein("cc", [128, 32])
        self.adab_in = ein("adab", [128, dep * 144])
        self.ng_in = ein("ng", [128, dep * 48])
        self.fg_in = ein("fg", [128, 16])
        self.lgd_in = ein("lgd", [128, self.nret * 32])
        self.cst_in = ein("cst", [128, 4 + 3 * 128])
        self.rope_in = ein("rope", [self.nlt, 128, 512])
        self.nab_in = ein("nab", [max(self.nna, 1) * NAH, 128, 5 * 896])
        self.w_ada = ein("ada_w", [dep, D, 9 * D])
        self.w_gu = ein("ffn_w_gu", [dep, 2, D, 2 * DFF])
        self.w_d = ein("ffn_w_down", [dep, 2, DFF, D])
        self.w_rin = ein("ret_w_in", [self.nret, D, 12288])
        self.w_rout = ein("ret_w_out", [self.nret, 4096, D])
        self.w_nin = ein("na_w_in", [max(self.nna, 1), D, 3 * D])
        self.w_nout = ein("na_w_out", [max(self.nna, 1), D, D])
        nn_ = max(self.nna, 1)
        self.b_ada = [scr("b_ada%d" % l, [D, 9 * D]) for l in range(dep)]
        self.b_gu = {(l, f): scr("b_gu%d_%d" % (l, f), [D, 2 * DFF]) for l in range(dep) for f in range(2)}
        self.b_d = {(l, f): scr("b_d%d_%d" % (l, f), [DFF, D]) for l in range(dep) for f in range(2)}
        self.b_rin = [scr("b_rin%d" % j, [D, 12288]) for j in range(self.nret)]
        self.b_rout = [scr("b_rout%d" % j, [4096, D]) for j in range(self.nret)]
        self.b_nin = [scr("b_nin%d" % j, [D, 3 * D]) for j in range(nn_)]
        self.b_nout = [scr("b_nout%d" % j, [D, D]) for j in range(nn_)]
        self.wres = {}
        self.XT = scr("XT", [D, ntok], F32)
        self.XT_r = [Res() for _ in self.blocks]
        self.QKVG = scr("QKVG", [ntok, 12288])
        self.QKVG_r = Res()
        self.OGT = scr("OGT", [4096, ntok])
        self.OGT_r = Res()

        A = self.A = Arena(nc, 50000)
        self.ps = [nc.alloc_psum_tensor("ps%d" % i, [128, 512], F32) for i in range(8)]
        self.ps_r = [Res() for _ in range(8)]
        self.c_r = Res()
        self.ones = A.alloc([128], BF16)
        self.ident = A.alloc([128], BF16)
        self.epsc = A.alloc([1], F32)
        self.cst = A.alloc([4 + 3 * 128], F32)
        self.mods = A.alloc([dep * 2 * 9 * 16], F32)
        self.mods_r = Res()
        self.fg = A.alloc([16], F32)
        P.op("dve", lambda e: e.memset(self.ones, 1.0 / D), writes=[self.c_r])
        P.op("dve", lambda e: e.memset(self.epsc, EPS), writes=[self.c_r])
        P.op("sp", lambda e: e.dma_start(out=self.cst, in_=self.cst_in), writes=[self.c_r], dma=True)
        P.op("sp", lambda e: e.dma_start(out=self.fg, in_=self.fg_in), writes=[self.c_r], dma=True)
        P.op("dve", lambda e: e.tensor_copy(out=self.ident, in_=self.cst[:, 4 + 256:4 + 384]),
             reads=[self.c_r], writes=[self.c_r])
        self.base_mark = A.mark()

        for bi, (t0, tb) in enumerate(self.blocks):
            P.op("sp", lambda e, t0=t0, tb=tb: e.dma_start(out=self.XT[:, t0:t0 + tb], in_=self.xin[:, t0:t0 + tb]),
                 writes=[self.XT_r[bi]], dma=True)

        self.cast_all()
        self.mod_setup()
        self.base_mark = A.mark()
        self.mod_phase(0)
        P.fence()
        outs = []
        for l in range(dep):
            self.stage_blocks(l)
            P.fence()
            m_pre = A.mark()
            if l + 1 < dep:
                self.mod_phase(l + 1, reset=False)
            if self.mixers[l] == "ret":
                self.ret_core(l)
            else:
                self.na_core(l)
            A.reset(m_pre)
            P.fence()
        outs = self.stage_blocks(dep)
        P.emit(final_waits=outs)
        return nc

    def cast(self, key, src, dst, rows, rstep=256):
        P = self.P
        rl = []
        for r0 in range(0, rows, rstep):
            r1 = min(rows, r0 + rstep)
            r = Res()
            P.op("pool", lambda e, r0=r0, r1=r1: e.dma_start(out=dst[r0:r1, :], in_=src[r0:r1, :], max_dma_last_dim=4096),
                 writes=[r], dma=True, bg=True)
            rl.append(r)
        self.wres[key] = rl

    def cast_all(self):
        for l in range(self.depth):
            j = l // 2
            self.cast(("ada", l), self.w_ada[l], self.b_ada[l], D)
            self.cast(("gu", l, 0), self.w_gu[l, 0], self.b_gu[(l, 0)], D)
            self.cast(("d", l, 0), self.w_d[l, 0], self.b_d[(l, 0)], DFF)
            if self.mixers[l] == "ret":
                self.cast(("min", l), self.w_rin[j], self.b_rin[j], D)
                self.cast(("mout", l), self.w_rout[j], self.b_rout[j], 4096)
            else:
                self.cast(("min", l), self.w_nin[j], self.b_nin[j], D)
                self.cast(("mout", l), self.w_nout[j], self.b_nout[j], D)
            self.cast(("gu", l, 1), self.w_gu[l, 1], self.b_gu[(l, 1)], D)
            self.cast(("d", l, 1), self.w_d[l, 1], self.b_d[(l, 1)], DFF)

    def modcol(self, l, who, k, which):
        o = (((l * 2 + who) * 3 + k) * 3 + which) * 16
        return self.mods[:, o:o + 16]

    def mod_setup(self):
        P, A = self.P, self.A
        dep = self.depth
        self.m_cc = A.alloc([32], F32)
        self.m_sc = A.alloc([16, 2], BF16)
        self.m_sg = A.alloc([32], F32)
        self.m_adab = A.alloc([dep * 144], F32)
        self.m_ng = A.alloc([dep * 48], F32)
        self.m_r = Res()
        cc, sc, sg, r = self.m_cc, self.m_sc, self.m_sg, self.m_r
        P.op("sp", lambda e: e.dma_start(out=cc, in_=self.cc_in), writes=[r], dma=True)
        P.op("sp", lambda e: e.dma_start(out=self.m_adab, in_=self.adab_in), writes=[r], dma=True)
        P.op("sp", lambda e: e.dma_start(out=self.m_ng, in_=self.ng_in), writes=[r], dma=True)
        P.op("act", lambda e: e.activation(out=sg, in_=cc, func=AF.Sigmoid), reads=[r], writes=[r])
        P.op("dve", lambda e: e.tensor_tensor(out=sc[:, :, 0], in0=cc[:, 0:16], in1=sg[:, 0:16], op=ALU.mult), reads=[r], writes=[r])
        P.op("dve", lambda e: e.tensor_tensor(out=sc[:, :, 1], in0=cc[:, 16:32], in1=sg[:, 16:32], op=ALU.mult), reads=[r], writes=[r])

    def mod_phase(self, l0, reset=True):
        P, A = self.P, self.A
        m0 = A.mark()
        sc, adab, ng, r = self.m_sc, self.m_adab, self.m_ng, self.m_r
        raw = A.alloc([2, 144], F32)
        wt = [A.alloc([NCH, 256], BF16) for _ in range(3)]
        wt_r = [Res() for _ in range(3)]
        wi = 0
        for l in [l0]:
            ps = self.ps[7]
            ps_r = self.ps_r[7]
            psv = ps[:, 0:288].rearrange("p (a b) -> p a b", b=2)
            for g in range(72):
                w, w_r = wt[wi % 3], wt_r[wi % 3]
                wi += 1
                src = self.b_ada[l][:, g * 256:(g + 1) * 256].rearrange("(c p) f -> p c f", p=128)
                P.op("sp", lambda e, w=w, src=src: e.dma_start(out=w, in_=src), reads=self.wres[("ada", l)], writes=[w_r], dma=True)
                for s in range(2):
                    cidx = g * 2 + s
                    for c in range(NCH):
                        P.op("pe", lambda e, w=w, s=s, c=c, cidx=cidx, psv=psv: e.matmul(
                            psv[:, cidx, :], lhsT=w[:, c, s * 128:(s + 1) * 128], rhs=sc[:, c, :],
                            start=(c == 0), stop=(c == NCH - 1)), reads=[w_r, r], writes=[ps_r])
            for who in range(2):
                P.op("dve", lambda e, who=who, psv=psv, l=l: e.tensor_tensor(
                    out=raw[:, who, :], in0=psv[:, :, who], in1=adab[:, l * 144:(l + 1) * 144], op=ALU.add),
                    reads=[ps_r, r], writes=[r])
                for k in range(3):
                    sh = raw[:, who, (3 * k) * 16:(3 * k + 1) * 16]
                    scl = raw[:, who, (3 * k + 1) * 16:(3 * k + 2) * 16]
                    gt = raw[:, who, (3 * k + 2) * 16:(3 * k + 3) * 16]
                    g_ = ng[:, (l * 3 + k) * 16:(l * 3 + k + 1) * 16]
                    P.op("dve", lambda e, l=l, who=who, k=k, scl=scl, g_=g_: e.scalar_tensor_tensor(
                        out=self.modcol(l, who, k, 0), in0=scl, scalar=1.0, in1=g_, op0=ALU.add, op1=ALU.mult),
                        reads=[r], writes=[self.mods_r])
                    P.op("dve", lambda e, l=l, who=who, k=k, sh=sh: e.tensor_copy(out=self.modcol(l, who, k, 1), in_=sh),
                         reads=[r], writes=[self.mods_r])
                    P.op("dve", lambda e, l=l, who=who, k=k, gt=gt: e.tensor_scalar(
                        out=self.modcol(l, who, k, 2), in0=gt, scalar1=(1.0 if k == 1 else 0.5), scalar2=None, op0=ALU.mult),
                        reads=[r], writes=[self.mods_r])
        if reset:
            A.reset(m0)

    def stage_blocks(self, l):
        P, A = self.P, self.A
        m0 = A.mark()
        S = type("S", (), {})()
        S.xT = A.alloc([NCH, 512], F32); S.xT_r = Res()
        S.hT = A.alloc([NCH, 512], BF16); S.hT_r = Res()
        S.aT = A.alloc([NF, 512], BF16); S.aT_r = [Res() for _ in range(NF)]
        S.rstd = A.alloc([512], F32); S.rstd_r = Res()
        S.tmp = [A.alloc([512], F32) for _ in range(3)]; S.tmp_r = [Res() for _ in range(3)]; S.ti = 0
        S.wgu = [(A.alloc([NCH, 256], BF16), Res()) for _ in range(4)]; S.wgi = 0
        S.wd = [(A.alloc([8, 512], BF16), Res()) for _ in range(3)]; S.wdi = 0
        S.rope = [(A.alloc([512], F32), Res()) for _ in range(4)]
        S.stg = [(A.alloc([256], BF16), Res()) for _ in range(4)]; S.si = 0
        S.ev = 0
        outs = []
        for bi, (t0, tb) in enumerate(self.blocks):
            who = 0 if t0 < self.nlat else 1
            src = self.XT[:, t0:t0 + tb].rearrange("(c p) t -> p c t", p=128)
            P.op("sp", lambda e, src=src, tb=tb: e.dma_start(out=S.xT[:, :, :tb], in_=src),
                 reads=[self.XT_r[bi]], writes=[S.xT_r], dma=True)
            if l > 0:
                pl = l - 1
                last = (pl == self.depth - 1)
                if not (last and who == 1):
                    kc = 32 if self.mixers[pl] == "ret" else 16
                    srco = self.OGT[0:kc * 128, t0:t0 + tb].rearrange("(c p) t -> p c t", p=128)
                    P.op("sp", lambda e, srco=srco, tb=tb, kc=kc: e.dma_start(out=S.aT[:, :kc, :tb], in_=srco),
                         reads=[self.OGT_r], writes=S.aT_r[:kc], dma=True)
                    wout = self.b_rout[pl // 2] if self.mixers[pl] == "ret" else self.b_nout[pl // 2]
                    self.down_proj(S, tb, wout, self.wres[("mout", pl)], kc, self.modcol(pl, who, 1, 2))
                    self.adanorm(S, tb, self.modcol(pl, who, 2, 0), self.modcol(pl, who, 2, 1))
                    self.ffn(S, tb, pl, 1, self.modcol(pl, who, 2, 2))
            if l < self.depth:
                self.adanorm(S, tb, self.modcol(l, who, 0, 0), self.modcol(l, who, 0, 1))
                self.ffn(S, tb, l, 0, self.modcol(l, who, 0, 2))
                self.adanorm(S, tb, self.modcol(l, who, 1, 0), self.modcol(l, who, 1, 1))
                self.in_proj(S, l, t0, tb, who)
                dst = self.XT[:, t0:t0 + tb].rearrange("(c p) t -> p c t", p=128)
                P.op("sp", lambda e, dst=dst, tb=tb: e.dma_start(out=dst, in_=S.xT[:, :, :tb]),
                     reads=[S.xT_r], writes=[self.XT_r[bi]], dma=True)
            elif who == 0:
                self.adanorm(S, tb, self.fg, None, final=True)
                dst = self.out[:, t0:t0 + tb].rearrange("(c p) t -> p c t", p=128)
                outs.append(P.op("sp", lambda e, dst=dst, tb=tb: e.dma_start(out=dst, in_=S.xT[:, :, :tb]),
                                 reads=[S.xT_r], writes=[Res()], dma=True))
        A.reset(m0)
        return outs

    def ntmp(self, S):
        i = S.ti % 3
        S.ti += 1
        return S.tmp[i], S.tmp_r[i]

    def adanorm(self, S, tb, Gc, Sc, final=False):
        P = self.P
        P.op("act", lambda e: e.activation(out=S.hT[:, :, :tb], in_=S.xT[:, :, :tb], func=AF.Square),
             reads=[S.xT_r], writes=[S.hT_r])
        ps, ps_r = self.ps[0], self.ps_r[0]
        for c in range(NCH):
            P.op("pe", lambda e, c=c: e.matmul(ps[:, :tb], lhsT=self.ones, rhs=S.hT[:, c, :tb],
                                                start=(c == 0), stop=(c == NCH - 1)),
                 reads=[self.c_r, S.hT_r], writes=[ps_r])
        P.op("act", lambda e: e.activation(out=S.rstd[:, :tb], in_=ps[:, :tb], func=AF.Sqrt, bias=self.epsc[:, 0:1]),
             reads=[ps_r, self.c_r], writes=[S.rstd_r])
        P.op("dve", lambda e: e.reciprocal(out=S.rstd[:, :tb], in_=S.rstd[:, :tb]), reads=[S.rstd_r], writes=[S.rstd_r])
        for c in range(NCH):
            if final:
                P.op("dve", lambda e, c=c: e.scalar_tensor_tensor(
                    out=S.xT[:, c, :tb], in0=S.xT[:, c, :tb], scalar=Gc[:, c:c + 1], in1=S.rstd[:, :tb],
                    op0=ALU.mult, op1=ALU.mult), reads=[S.xT_r, S.rstd_r, self.c_r], writes=[S.xT_r])
                continue
            tmp, tmp_r = self.ntmp(S)
            P.op("dve", lambda e, c=c, tmp=tmp: e.scalar_tensor_tensor(
                out=tmp[:, :tb], in0=S.xT[:, c, :tb], scalar=Gc[:, c:c + 1], in1=S.rstd[:, :tb],
                op0=ALU.mult, op1=ALU.mult), reads=[S.xT_r, S.rstd_r, self.mods_r], writes=[tmp_r])
            P.op("act", lambda e, c=c, tmp=tmp: e.activation(
                out=S.hT[:, c, :tb], in_=tmp[:, :tb], func=AF.Identity, bias=Sc[:, c:c + 1]),
                reads=[tmp_r, self.mods_r], writes=[S.hT_r])

    def ffn(self, S, tb, l, f, HGc):
        P = self.P
        Wgu, Wd = self.b_gu[(l, f)], self.b_d[(l, f)]
        gu_r, d_r = self.wres[("gu", l, f)], self.wres[("d", l, f)]
        bank = 0
        for g in range((NF + 1) // 2):
            nf = min(2, NF - g * 2)
            ncol = nf * 128
            wg, wg_r = S.wgu[S.wgi % 4]; S.wgi += 1
            wu, wu_r = S.wgu[S.wgi % 4]; S.wgi += 1
            c0 = g * 256
            srcg = Wgu[:, c0:c0 + ncol].rearrange("(c p) f -> p c f", p=128)
            srcu = Wgu[:, DFF + c0:DFF + c0 + ncol].rearrange("(c p) f -> p c f", p=128)
            P.op("sp", lambda e, wg=wg, srcg=srcg, ncol=ncol: e.dma_start(out=wg[:, :, :ncol], in_=srcg),
                 reads=gu_r, writes=[wg_r], dma=True)
            P.op("sp", lambda e, wu=wu, srcu=srcu, ncol=ncol: e.dma_start(out=wu[:, :, :ncol], in_=srcu),
                 reads=gu_r, writes=[wu_r], dma=True)
            for fl in range(nf):
                fi = g * 2 + fl
                bg = 1 + bank % 3; bu = 1 + (bank + 1) % 3; bank += 2
                if bg == bu:
                    bu = 1 + (bank) % 3
                for c in range(NCH):
                    P.op("pe", lambda e, c=c, fl=fl, wg=wg, bg=bg: e.matmul(
                        self.ps[bg][:, :tb], lhsT=wg[:, c, fl * 128:(fl + 1) * 128], rhs=S.hT[:, c, :tb],
                        start=(c == 0), stop=(c == NCH - 1)), reads=[wg_r, S.hT_r], writes=[self.ps_r[bg]])
                for c in range(NCH):
                    P.op("pe", lambda e, c=c, fl=fl, wu=wu, bu=bu: e.matmul(
                        self.ps[bu][:, :tb], lhsT=wu[:, c, fl * 128:(fl + 1) * 128], rhs=S.hT[:, c, :tb],
                        start=(c == 0), stop=(c == NCH - 1)), reads=[wu_r, S.hT_r], writes=[self.ps_r[bu]])
                tmp, tmp_r = self.ntmp(S)
                P.op("act", lambda e, tmp=tmp, bg=bg: e.activation(out=tmp[:, :tb], in_=self.ps[bg][:, :tb], func=AF.Silu),
                     reads=[self.ps_r[bg]], writes=[tmp_r])
                P.op("dve", lambda e, tmp=tmp, bu=bu, fi=fi: e.tensor_tensor(
                    out=S.aT[:, fi, :tb], in0=tmp[:, :tb], in1=self.ps[bu][:, :tb], op=ALU.mult),
                    reads=[tmp_r, self.ps_r[bu]], writes=[S.aT_r[fi]])
        self.down_proj(S, tb, Wd, d_r, NF, HGc)

    def down_proj(self, S, tb, W, W_r, kc, HGc):
        P = self.P
        for q in range(4):
            for g in range((kc + 7) // 8):
                nf = min(8, kc - g * 8)
                wd, wd_r = S.wd[S.wdi % 3]; S.wdi += 1
                src = W[g * 1024:(g * 8 + nf) * 128, q * 512:(q + 1) * 512].rearrange("(f p) n -> p f n", p=128)
                P.op("sp", lambda e, wd=wd, src=src, nf=nf: e.dma_start(out=wd[:, :nf, :], in_=src),
                     reads=W_r, writes=[wd_r], dma=True)
                for fl in range(nf):
                    fi = g * 8 + fl
                    for j in range(4):
                        P.op("pe", lambda e, wd=wd, fl=fl, j=j, fi=fi: e.matmul(
                            self.ps[4 + j][:, :tb], lhsT=wd[:, fl, j * 128:(j + 1) * 128], rhs=S.aT[:, fi, :tb],
                            start=(fi == 0), stop=(fi == kc - 1)), reads=[wd_r, S.aT_r[fi]], writes=[self.ps_r[4 + j]])
            for j in range(4):
                c = q * 4 + j
                P.op("dve", lambda e, j=j, c=c: e.scalar_tensor_tensor(
                    out=S.xT[:, c, :tb], in0=self.ps[4 + j][:, :tb], scalar=HGc[:, c:c + 1], in1=S.xT[:, c, :tb],
                    op0=ALU.mult, op1=ALU.add), reads=[self.ps_r[4 + j], S.xT_r, self.mods_r], writes=[S.xT_r])

    def in_proj(self, S, l, t0, tb, who):
        P = self.P
        ret = self.mixers[l] == "ret"
        W = self.b_rin[l // 2] if ret else self.b_nin[l // 2]
        W_r = self.wres[("min", l)]
        ncols = 12288 if ret else 3 * D
        ntile = tb // 128
        if ret and who == 0:
            for ti in range(ntile):
                rp, rp_r = S.rope[ti]
                P.op("sp", lambda e, rp=rp, ti=ti: e.dma_start(out=rp, in_=self.rope_in[t0 // 128 + ti]),
                     writes=[rp_r], dma=True)
        for g in range(ncols // 256):
            w, w_r = S.wgu[S.wgi % 4]; S.wgi += 1
            src = W[:, g * 256:(g + 1) * 256].rearrange("(c p) f -> p c f", p=128)
            P.op("sp", lambda e, w=w, src=src: e.dma_start(out=w, in_=src), reads=W_r, writes=[w_r], dma=True)
            col0 = g * 256
            kind = "copy"
            if ret and col0 < 4096:
                kind = "q" if col0 < 2048 else "k"
            elif ret and col0 >= 8192:
                kind = "silu"
            for ti in range(ntile):
                b = 1 + (S.ev % 3)
                ps, ps_r = self.ps[b], self.ps_r[b]
                for c in range(NCH):
                    P.op("pe", lambda e, c=c, ti=ti, w=w, ps=ps: e.matmul(
                        ps[:, :256], lhsT=S.hT[:, c, ti * 128:(ti + 1) * 128], rhs=w[:, c, :],
                        start=(c == 0), stop=(c == NCH - 1)), reads=[w_r, S.hT_r], writes=[ps_r])
                stg, stg_r = S.stg[S.si % 4]; S.si += 1
                if kind == "silu":
                    P.op("act", lambda e, stg=stg, ps=ps: e.activation(out=stg, in_=ps[:, :256], func=AF.Silu),
                         reads=[ps_r], writes=[stg_r])
                elif kind == "copy" or who == 1:
                    sc = 0.0625 if kind == "k" else 1.0
                    if S.ev % 2 == 0:
                        P.op("act", lambda e, stg=stg, ps=ps, sc=sc: e.activation(out=stg, in_=ps[:, :256], func=AF.Copy, scale=sc),
                             reads=[ps_r], writes=[stg_r])
                    else:
                        P.op("dve", lambda e, stg=stg, ps=ps, sc=sc: e.tensor_scalar(
                            out=stg, in0=ps[:, :256], scalar1=sc, scalar2=None, op0=ALU.mult), reads=[ps_r], writes=[stg_r])
                else:
                    sc = 0.0625 if kind == "k" else 1.0
                    rp, rp_r = S.rope[ti]
                    t1, t1_r = self.ntmp(S)
                    t2, t2_r = self.ntmp(S)
                    P.op("dve", lambda e, t1=t1, ps=ps, rp=rp: e.tensor_tensor(
                        out=t1[:, 0:256], in0=ps[:, :256], in1=rp[:, 0:256], op=ALU.mult), reads=[ps_r, rp_r], writes=[t1_r])
                    psv = ps[:, :256].rearrange("p (a h x) -> p a h x", a=2, h=2)
                    t2v = t2[:, 0:256].rearrange("p (a h x) -> p a h x", a=2, h=2)
                    rpv = rp[:, 256:512].rearrange("p (a h x) -> p a h x", a=2, h=2)
                    P.op("dve", lambda e, t2v=t2v, psv=psv, rpv=rpv: e.tensor_tensor(
                        out=t2v[:, :, 0, :], in0=psv[:, :, 1, :], in1=rpv[:, :, 0, :], op=ALU.mult), reads=[ps_r, rp_r], writes=[t2_r])
                    P.op("dve", lambda e, t2v=t2v, psv=psv, rpv=rpv: e.tensor_tensor(
                        out=t2v[:, :, 1, :], in0=psv[:, :, 0, :], in1=rpv[:, :, 1, :], op=ALU.mult), reads=[ps_r, rp_r], writes=[t2_r])
                    P.op("dve", lambda e, stg=stg, t1=t1, t2=t2, sc=sc: e.scalar_tensor_tensor(
                        out=stg, in0=t1[:, 0:256], scalar=sc, in1=t2[:, 0:256], op0=ALU.mult, op1=ALU.add) if sc == 1.0 else
                        e.tensor_tensor(out=stg, in0=t1[:, 0:256], in1=t2[:, 0:256], op=ALU.add),
                        reads=[t1_r, t2_r], writes=[stg_r])
                    if sc != 1.0:
                        P.op("act", lambda e, stg=stg, sc=sc: e.activation(out=stg, in_=stg, func=AF.Copy, scale=sc),
                             reads=[stg_r], writes=[stg_r])
                S.ev += 1
                tok0 = t0 + ti * 128
                P.op("sp", lambda e, stg=stg, tok0=tok0, col0=col0: e.dma_start(
                    out=self.QKVG[tok0:tok0 + 128, col0:col0 + 256], in_=stg), reads=[stg_r], writes=[Res()], dma=True)

    def transpose_to(self, dst, dst_r, src, src_r, n, evac):
        P = self.P
        b = self.tb_bank
        self.tb_bank = 1 - self.tb_bank
        psb = self.ps[b][:, :].bitcast(BF16)
        for i in range(n):
            P.op("pe", lambda e, i=i, psb=psb: e.transpose(out=psb[:, i * 128:(i + 1) * 128], in_=src[:, i * 128:(i + 1) * 128],
                                                          identity=self.ident), reads=[src_r, self.c_r], writes=[self.ps_r[b]])
        if evac == "act":
            P.op("act", lambda e, psb=psb: e.activation(out=dst, in_=psb[:, :n * 128], func=AF.Copy), reads=[self.ps_r[b]], writes=[dst_r])
        else:
            P.op("dve", lambda e, psb=psb: e.tensor_copy(out=dst, in_=psb[:, :n * 128]), reads=[self.ps_r[b]], writes=[dst_r])

    def ret_core(self, l):
        P, A = self.P, self.A
        m0 = A.mark()
        self.tb_bank = 0
        j = l // 2
        nt, nlt = self.nt, self.nlt
        need_ctx = l != self.depth - 1
        r = Res()
        lgd = A.alloc([32], F32)
        LG = A.alloc([16], F32)
        dec = A.alloc([4, 8], F32)
        cdec = A.alloc([2, 8], F32)
        MT = A.alloc([8, 128], F32)
        t128 = A.alloc([128], F32)
        pos = self.cst[:, 0:4]
        D1 = self.cst[:, 4:132]; D2 = self.cst[:, 132:260]; I_ = self.cst[:, 260:388]
        P.op("sp", lambda e: e.dma_start(out=lgd, in_=self.lgd_in[:, j * 32:(j + 1) * 32]), writes=[r], dma=True)
        P.op("act", lambda e: e.activation(out=LG, in_=lgd[:, 0:16], func=AF.Exp), reads=[r], writes=[r])
        P.op("dve", lambda e: e.tensor_scalar(out=LG, in0=LG, scalar1=-1.0, scalar2=None, op0=ALU.mult), reads=[r], writes=[r])
        for k, (pc, off) in enumerate([(0, 0), (1, 0), (3, 8), (2, 8)]):
            P.op("dve", lambda e, k=k, pc=pc, off=off: e.tensor_scalar(
                out=dec[:, k, :], in0=LG[:, off:off + 8], scalar1=pos[:, pc:pc + 1], scalar2=None, op0=ALU.mult),
                reads=[r, self.c_r], writes=[r])
        P.op("act", lambda e: e.activation(out=dec, in_=dec, func=AF.Exp), reads=[r], writes=[r])
        P.op("act", lambda e: e.activation(out=cdec, in_=LG.rearrange("p (a b) -> p a b", b=8), func=AF.Exp, scale=128.0), reads=[r], writes=[r])
        for h in range(8):
            P.op("dve", lambda e, h=h: e.tensor_scalar(out=t128, in0=D1, scalar1=LG[:, h:h + 1], scalar2=None, op0=ALU.mult),
                 reads=[r, self.c_r], writes=[r])
            P.op("dve", lambda e, h=h: e.scalar_tensor_tensor(out=t128, in0=D2, scalar=LG[:, 8 + h:9 + h], in1=t128,
                                                               op0=ALU.mult, op1=ALU.add), reads=[r, self.c_r], writes=[r])
            P.op("act", lambda e: e.activation(out=t128, in_=t128, func=AF.Exp), reads=[r], writes=[r])
            P.op("dve", lambda e, h=h: e.tensor_tensor(out=MT[:, h, :], in0=t128, in1=I_, op=ALU.add), reads=[r, self.c_r], writes=[r])

        kk = A.alloc([nt, 256], BF16); kk_r = Res()
        vv = A.alloc([nt, 512], BF16); vv_r = Res()
        Sb = A.alloc([nlt + 1, 2, 512], BF16); Sb_r = [Res() for _ in range(nlt + 1)]
        Sm = A.alloc([2, 512], F32); Sm_r = Res()
        Sf = A.alloc([2, 512], BF16); Sf_r = Res()
        ring = lambda shape, dt, n: [(A.alloc(shape, dt), Res()) for _ in range(n)]
        qb = ring([256], BF16, 3); gb = ring([512], BF16, 3)
        kh = ring([256], BF16, 3)
        q3 = ring([3, 256], BF16, 3)
        qT = ring([3 * 256], BF16, 2)
        kT = ring([256], BF16, 2)
        sT = ring([128], BF16, 2)
        og = ring([512], BF16, 2)
        ogT = ring([512], BF16, 2)
        st = ring([8], F32, 2)
        onr = ring([512], F32, 2)
        it = {}

        def nxt(rg):
            k = it.get(id(rg), 0)
            it[id(rg)] = k + 1
            return rg[k % len(rg)]

        def state_update(c, h, kdidx, cidx):
            khb, khb_r = nxt(kh)
            P.op("dve", lambda e, khb=khb, c=c, h=h: e.tensor_scalar(
                out=khb, in0=kk[:, c, :], scalar1=dec[:, kdidx, h:h + 1], scalar2=None, op0=ALU.mult),
                reads=[kk_r, r], writes=[khb_r])
            for dc in range(2):
                P.op("pe", lambda e, dc=dc, khb=khb, c=c: e.matmul(
                    self.ps[2 + dc][:, :], lhsT=khb[:, dc * 128:(dc + 1) * 128], rhs=vv[:, c, :], start=True, stop=True),
                    reads=[khb_r, vv_r], writes=[self.ps_r[2 + dc]])
            for dc in range(2):
                P.op("dve", lambda e, dc=dc, h=h: e.scalar_tensor_tensor(
                    out=Sm[:, dc, :], in0=Sm[:, dc, :], scalar=cdec[:, cidx, h:h + 1], in1=self.ps[2 + dc][:, :],
                    op0=ALU.mult, op1=ALU.add), reads=[Sm_r, self.ps_r[2 + dc], r], writes=[Sm_r])

        for h in range(8):
            srck = self.QKVG[:, 2048 + h * 256:2048 + (h + 1) * 256].rearrange("(t p) f -> p t f", p=128)
            srcv = self.QKVG[:, 4096 + h * 512:4096 + (h + 1) * 512].rearrange("(t p) f -> p t f", p=128)
            P.op("sp", lambda e, srck=srck: e.dma_start(out=kk, in_=srck), reads=[self.QKVG_r], writes=[kk_r], dma=True)
            P.op("sp", lambda e, srcv=srcv: e.dma_start(out=vv, in_=srcv), reads=[self.QKVG_r], writes=[vv_r], dma=True)
            P.op("dve", lambda e: e.memset(Sm, 0.0), writes=[Sm_r])
            order = [nlt + 1, nlt] + list(range(nlt - 1, 0, -1))
            slot_of = {nlt: 0}
            for c in range(nlt):
                slot_of[c] = 1 + c
            for c in order:
                state_update(c, h, 3, 1)
                nxt_c = nlt if c == nlt + 1 else (nlt - 1 if c == nlt else c - 1)
                sl = slot_of[nxt_c]
                P.op("act", lambda e, sl=sl: e.activation(out=Sb[:, sl, :, :], in_=Sm, func=AF.Copy),
                     reads=[Sm_r], writes=[Sb_r[sl]])
            P.op("dve", lambda e: e.memset(Sm, 0.0), writes=[Sm_r])
            forder = [nlt, nlt + 1] + list(range(nlt))

            def stage_a(idx, c):
                is_ctx = c >= nlt
                if is_ctx and not need_ctx:
                    return None
                gbb, gb_r = nxt(gb)
                q3b, q3_r = nxt(q3)
                P.op("sp", lambda e, q3b=q3b, c=c, h=h: e.dma_start(
                    out=q3b[:, 0, :], in_=self.QKVG[c * 128:(c + 1) * 128, h * 256:(h + 1) * 256]),
                    writes=[q3_r], dma=True)
                P.op("sp", lambda e, gbb=gbb, c=c, h=h: e.dma_start(
                    out=gbb, in_=self.QKVG[c * 128:(c + 1) * 128, 8192 + h * 512:8192 + (h + 1) * 512]),
                    writes=[gb_r], dma=True)
                P.op("dve", lambda e, q3b=q3b, h=h: e.tensor_scalar(
                    out=q3b[:, 1, :], in0=q3b[:, 0, :], scalar1=dec[:, 0, h:h + 1], scalar2=None, op0=ALU.mult),
                    reads=[q3_r, r], writes=[q3_r])
                P.op("dve", lambda e, q3b=q3b, h=h: e.tensor_scalar(
                    out=q3b[:, 2, :], in0=q3b[:, 0, :], scalar1=dec[:, 2, h:h + 1], scalar2=None, op0=ALU.mult),
                    reads=[q3_r, r], writes=[q3_r])
                qTb, qT_r = nxt(qT)
                self.transpose_to(qTb, qT_r, q3b.rearrange("p a b -> p (a b)"), q3_r, 6, "act")
                kTb, kT_r = nxt(kT)
                self.transpose_to(kTb, kT_r, kk[:, c, :], kk_r, 2, "dve")
                for dc in range(2):
                    P.op("pe", lambda e, dc=dc, kTb=kTb, qTb=qTb: e.matmul(
                        self.ps[4][:, 0:128], lhsT=kTb[:, dc * 128:(dc + 1) * 128], rhs=qTb[:, dc * 128:(dc + 1) * 128],
                        start=(dc == 0), stop=(dc == 1)), reads=[kT_r, qT_r], writes=[self.ps_r[4]])
                sTb, sT_r = nxt(sT)
                P.op("dve", lambda e, sTb=sTb, h=h: e.tensor_tensor(out=sTb, in0=self.ps[4][:, 0:128], in1=MT[:, h, :], op=ALU.mult),
                     reads=[self.ps_r[4], r], writes=[sT_r])
                return (qTb, qT_r, sTb, sT_r, gbb, gb_r)

            def stage_b(idx, c, cx):
                qTb, qT_r, sTb, sT_r, gbb, gb_r = cx
                use_f = idx > 0
                use_b = c != nlt + 1
                terms = [("s", 0)]
                if use_f:
                    terms += [("f", 0), ("f", 1)]
                if use_b:
                    terms += [("b", 0), ("b", 1)]
                pso, pso_r = self.ps[5 + (idx % 2)], self.ps_r[5 + (idx % 2)]
                sl = slot_of[c] if use_b else 0
                for ti_, (kind, dc) in enumerate(terms):
                    st_, sp_ = (ti_ == 0), (ti_ == len(terms) - 1)
                    if kind == "s":
                        P.op("pe", lambda e, sTb=sTb, c=c, pso=pso, st_=st_, sp_=sp_: e.matmul(
                            pso[:, :], lhsT=sTb, rhs=vv[:, c, :], start=st_, stop=sp_), reads=[sT_r, vv_r], writes=[pso_r])
                    elif kind == "f":
                        P.op("pe", lambda e, qTb=qTb, dc=dc, pso=pso, st_=st_, sp_=sp_: e.matmul(
                            pso[:, :], lhsT=qTb[:, 256 + dc * 128:256 + (dc + 1) * 128], rhs=Sf[:, dc, :], start=st_, stop=sp_),
                            reads=[qT_r, Sf_r], writes=[pso_r])
                    else:
                        P.op("pe", lambda e, qTb=qTb, dc=dc, pso=pso, sl=sl, st_=st_, sp_=sp_: e.matmul(
                            pso[:, :], lhsT=qTb[:, 512 + dc * 128:512 + (dc + 1) * 128], rhs=Sb[:, sl, dc, :], start=st_, stop=sp_),
                            reads=[qT_r, Sb_r[sl]], writes=[pso_r])
                stb, st_r = nxt(st)
                onb, on_r = nxt(onr)
                ogb, og_r = nxt(og)
                P.op("act", lambda e, onb=onb, pso=pso, stb=stb: e.activation(out=onb, in_=pso[:, :], func=AF.Copy, accum_out=stb[:, 0:1]),
                     reads=[pso_r], writes=[on_r, st_r])
                P.op("dve", lambda e, stb=stb: e.tensor_scalar(out=stb[:, 1:2], in0=stb[:, 0:1], scalar1=-1.0 / 512, scalar2=None, op0=ALU.mult),
                     reads=[st_r], writes=[st_r])
                P.op("act", lambda e, ogb=ogb, onb=onb, stb=stb: e.activation(out=ogb, in_=onb, func=AF.Square, bias=stb[:, 1:2], accum_out=stb[:, 2:3]),
                     reads=[on_r, st_r], writes=[og_r, st_r])
                P.op("act", lambda e, stb=stb: e.activation(out=stb[:, 3:4], in_=stb[:, 2:3], func=AF.Sqrt, scale=1.0 / 512, bias=self.epsc[:, 0:1]),
                     reads=[st_r, self.c_r], writes=[st_r])
                P.op("dve", lambda e, stb=stb: e.reciprocal(out=stb[:, 3:4], in_=stb[:, 3:4]), reads=[st_r], writes=[st_r])
                P.op("dve", lambda e, onb=onb, stb=stb: e.tensor_scalar(
                    out=onb, in0=onb, scalar1=stb[:, 1:2], scalar2=stb[:, 3:4], op0=ALU.add, op1=ALU.mult),
                    reads=[on_r, st_r], writes=[on_r])
                P.op("dve", lambda e, ogb=ogb, onb=onb, gbb=gbb: e.tensor_tensor(out=ogb, in0=onb, in1=gbb, op=ALU.mult),
                     reads=[on_r, gb_r], writes=[og_r])
                ogTb, ogT_r = nxt(ogT)
                self.transpose_to(ogTb, ogT_r, ogb, og_r, 4, "act")
                dst = self.OGT[h * 512:(h + 1) * 512, c * 128:(c + 1) * 128].rearrange("(k p) t -> p k t", p=128)
                P.op("sp", lambda e, dst=dst, ogTb=ogTb: e.dma_start(out=dst, in_=ogTb.rearrange("p (k t) -> p k t", t=128)),
                     reads=[ogT_r], writes=[Res()], dma=True)

            pend = stage_a(0, forder[0])
            for idx, c in enumerate(forder):
                nx_ = stage_a(idx + 1, forder[idx + 1]) if idx + 1 < len(forder) else None
                if pend is not None:
                    stage_b(idx, c, pend)
                if idx < len(forder) - 1:
                    state_update(c, h, 1, 0)
                    P.op("act", lambda e: e.activation(out=Sf, in_=Sm, func=AF.Copy), reads=[Sm_r], writes=[Sf_r])
                pend = nx_
        A.reset(m0)

    def na_core(self, l):
        P, A = self.P, self.A
        m0 = A.mark()
        self.tb_bank = 0
        j = l // 2
        nt, nlt = self.nt, self.nlt
        need_ctx = l != self.depth - 1
        scale = NADH ** -0.5
        ring = lambda shape, dt, n: [(A.alloc(shape, dt), Res()) for _ in range(n)]
        qk = ring([nt, 128], BF16, 2)
        vv = ring([nt, 128], BF16, 2)
        qTh = ring([nt, 128], BF16, 2)
        kTh = ring([nt, 128], BF16, 2)
        oTh = ring([nt * 128], BF16, 2)
        nab = ring([5, 896], F32, 2)
        ssb = ring([896], F32, 2)
        pb = ring([896], BF16, 2)
        pT = ring([896], BF16, 2)
        mx = ring([4], F32, 3)
        ob = ring([128], BF16, 2)
        it = {}

        def nxt(rg):
            k = it.get(id(rg), 0)
            it[id(rg)] = k + 1
            return rg[k % len(rg)]

        for h in range(NAH):
            qtm, qtm_r = nxt(qk)
            srcq = self.QKVG[:, h * 128:(h + 1) * 128].rearrange("(t p) f -> p t f", p=128)
            P.op("sp", lambda e, qtm=qtm, srcq=srcq: e.dma_start(out=qtm, in_=srcq), reads=[self.QKVG_r], writes=[qtm_r], dma=True)
            ktm, ktm_r = nxt(qk)
            srck = self.QKVG[:, D + h * 128:D + (h + 1) * 128].rearrange("(t p) f -> p t f", p=128)
            P.op("sp", lambda e, ktm=ktm, srck=srck: e.dma_start(out=ktm, in_=srck), reads=[self.QKVG_r], writes=[ktm_r], dma=True)
            vb, vb_r = nxt(vv)
            srcv = self.QKVG[:, 2 * D + h * 128:2 * D + (h + 1) * 128].rearrange("(t p) f -> p t f", p=128)
            P.op("sp", lambda e, vb=vb, srcv=srcv: e.dma_start(out=vb, in_=srcv), reads=[self.QKVG_r], writes=[vb_r], dma=True)
            nb, nb_r = nxt(nab)
            P.op("sp", lambda e, nb=nb, h=h: e.dma_start(out=nb, in_=self.nab_in[j * NAH + h].rearrange("p (a b) -> p a b", b=896)),
                 writes=[nb_r], dma=True)
            qT, qT_r = nxt(qTh)
            kT, kT_r = nxt(kTh)
            for t0 in range(0, nt, 8):
                n = min(8, nt - t0)
                self.transpose_to(qT[:, t0:t0 + n, :].rearrange("p a b -> p (a b)"), qT_r,
                                  qtm[:, t0:t0 + n, :].rearrange("p a b -> p (a b)"), qtm_r, n, "act")
                self.transpose_to(kT[:, t0:t0 + n, :].rearrange("p a b -> p (a b)"), kT_r,
                                  ktm[:, t0:t0 + n, :].rearrange("p a b -> p (a b)"), ktm_r, n, "dve")
            oT, oT_r = nxt(oTh)
            ntq = nt if need_ctx else nlt
            def stage_a(t):
                is_ctx = t >= nlt
                if is_ctx:
                    keyt = [nlt, nlt + 1]
                    nloc = 0
                    w = ty = 0
                else:
                    w = min(max(t - 2, 0), nlt - 5)
                    keyt = [w + i for i in range(5)] + [nlt, nlt + 1]
                    nloc = 640
                    ty = 0 if t == 0 else 1 if t == 1 else 3 if t == nlt - 2 else 4 if t == nlt - 1 else 2
                nk = len(keyt) * 128
                b0, b1 = (2, 3) if t % 2 == 0 else (4, 5)
                s_b, s_r = nxt(ssb)
                if not is_ctx:
                    P.op("pe", lambda e, t=t, w=w, b0=b0, qT=qT, kT=kT: e.matmul(
                        self.ps[b0][:, 0:512], lhsT=qT[:, t, :], rhs=kT[:, w:w + 4, :].rearrange("p a b -> p (a b)"), start=True, stop=True),
                        reads=[qT_r, kT_r], writes=[self.ps_r[b0]])
                    P.op("pe", lambda e, t=t, w=w, b1=b1, qT=qT, kT=kT: e.matmul(
                        self.ps[b1][:, 0:128], lhsT=qT[:, t, :], rhs=kT[:, w + 4, :], start=True, stop=True),
                        reads=[qT_r, kT_r], writes=[self.ps_r[b1]])
                P.op("pe", lambda e, t=t, b1=b1, qT=qT, kT=kT: e.matmul(
                    self.ps[b1][:, 128:384], lhsT=qT[:, t, :], rhs=kT[:, nlt:nlt + 2, :].rearrange("p a b -> p (a b)"), start=True, stop=True),
                    reads=[qT_r, kT_r], writes=[self.ps_r[b1]])
                if not is_ctx:
                    P.op("dve", lambda e, s_b=s_b, b0=b0, nb=nb, ty=ty: e.scalar_tensor_tensor(
                        out=s_b[:, 0:512], in0=self.ps[b0][:, 0:512], scalar=scale, in1=nb[:, ty, 0:512], op0=ALU.mult, op1=ALU.add),
                        reads=[self.ps_r[b0], nb_r], writes=[s_r])
                    P.op("dve", lambda e, s_b=s_b, b1=b1, nb=nb, ty=ty: e.scalar_tensor_tensor(
                        out=s_b[:, 512:896], in0=self.ps[b1][:, 0:384], scalar=scale, in1=nb[:, ty, 512:896], op0=ALU.mult, op1=ALU.add),
                        reads=[self.ps_r[b1], nb_r], writes=[s_r])
                else:
                    P.op("act", lambda e, s_b=s_b, b1=b1: e.activation(
                        out=s_b[:, 0:256], in_=self.ps[b1][:, 128:384], func=AF.Copy, scale=scale),
                        reads=[self.ps_r[b1]], writes=[s_r])
                m_, m_r = nxt(mx)
                P.op("dve", lambda e, m_=m_, s_b=s_b, nk=nk: e.tensor_reduce(out=m_[:, 1:2], in_=s_b[:, 0:nk], axis=AX.X, op=ALU.max, negate=True),
                     reads=[s_r], writes=[m_r])
                p_, p_r = nxt(pb)
                P.op("act", lambda e, p_=p_, s_b=s_b, m_=m_, nk=nk: e.activation(
                    out=p_[:, 0:nk], in_=s_b[:, 0:nk], func=AF.Exp, bias=m_[:, 1:2], accum_out=m_[:, 2:3]),
                    reads=[s_r, m_r], writes=[p_r, m_r])
                P.op("dve", lambda e, m_=m_: e.reciprocal(out=m_[:, 3:4], in_=m_[:, 2:3]), reads=[m_r], writes=[m_r])
                return (t, keyt, nk, p_, p_r, m_, m_r)

            def stage_b(cx):
                t, keyt, nk, p_, p_r, m_, m_r = cx
                pT_, pT_r = nxt(pT)
                self.transpose_to(pT_[:, 0:nk], pT_r, p_[:, 0:nk], p_r, len(keyt), "act")
                pso, pso_r = self.ps[6 + t % 2], self.ps_r[6 + t % 2]
                for ki, kt in enumerate(keyt):
                    P.op("pe", lambda e, ki=ki, kt=kt, pT_=pT_, vb=vb, pso=pso: e.matmul(
                        pso[:, 0:128], lhsT=pT_[:, ki * 128:(ki + 1) * 128], rhs=vb[:, kt, :],
                        start=(ki == 0), stop=(ki == len(keyt) - 1)), reads=[pT_r, vb_r], writes=[pso_r])
                o_, o_r = nxt(ob)
                P.op("dve", lambda e, o_=o_, pso=pso, m_=m_: e.tensor_scalar(
                    out=o_, in0=pso[:, 0:128], scalar1=m_[:, 3:4], scalar2=None, op0=ALU.mult),
                    reads=[pso_r, m_r], writes=[o_r])
                self.transpose_to(oT[:, t * 128:(t + 1) * 128], oT_r, o_, o_r, 1, "act" if t % 2 else "dve")

            pend = stage_a(0)
            for t in range(ntq):
                nx_ = stage_a(t + 1) if t + 1 < ntq else None
                stage_b(pend)
                pend = nx_
            ntk = ntq * 128
            P.op("sp", lambda e, oT=oT, h=h, ntk=ntk: e.dma_start(out=self.OGT[h * 128:(h + 1) * 128, 0:ntk], in_=oT[:, 0:ntk]),
                 reads=[oT_r], writes=[Res()], dma=True)
        A.reset(m0)


def _col(v):
    v = np.asarray(v, np.float32)
    lead = v.shape[:-1]
    return np.ascontiguousarray(np.moveaxis(v.reshape(lead + (16, 128)), -1, 0))


def _rope_tables(nlat):
    t = np.arange(nlat)
    row = (t // GRID_W).astype(np.float32)
    col = (t % GRID_W).astype(np.float32)
    half = RDK // 2
    freqs = (10000.0 ** (-np.arange(0, half, 2, dtype=np.float32) / half)).astype(np.float32)
    ar = row[:, None] * freqs
    ac = col[:, None] * freqs
    cos = np.concatenate([np.cos(ar), np.cos(ar), np.cos(ac), np.cos(ac)], -1)
    sin = np.concatenate([-np.sin(ar), np.sin(ar), -np.sin(ac), np.sin(ac)], -1)
    tab = np.concatenate([cos, sin], -1).astype(np.float32)
    return np.ascontiguousarray(tab.reshape(nlat // 128, 128, 512))


def _na_bias(rpb, rows):
    nlt = rows // 2
    nn, H = rpb.shape[0], rpb.shape[1]
    out = np.full((nn * H, 128, 5, 896), -30000.0, np.float32)
    out[:, :, :, 640:] = 0.0
    types = [0, 1, 2, nlt - 2, nlt - 1]
    qa = np.arange(128) // 64
    qc = np.arange(128) % 64
    ku = np.arange(640) // 64
    kc = np.arange(640) % 64
    cs = np.clip(qc - NA_KC // 2, 0, GRID_W - NA_KC)
    for ti, t in enumerate(types):
        w = min(max(t - 2, 0), nlt - 5)
        r = 2 * t + qa
        rs = np.clip(r - NA_KR // 2, 0, rows - NA_KR)
        krow = 2 * w + ku
        okr = (krow[None, :] >= rs[:, None]) & (krow[None, :] < rs[:, None] + NA_KR)
        okc = (kc[None, :] >= cs[:, None]) & (kc[None, :] < cs[:, None] + NA_KC)
        ok = okr & okc
        dr = np.clip(krow[None, :] - r[:, None] + (NA_KR - 1), 0, 2 * NA_KR - 2)
        dc = np.clip(kc[None, :] - qc[:, None] + (NA_KC - 1), 0, 2 * NA_KC - 2)
        for n in range(nn):
            g = rpb[n][:, dr, dc]
            out[n * H:(n + 1) * H, :, ti, :640] = np.where(ok[None], g, np.float32(-30000.0))
    return np.ascontiguousarray(out.reshape(nn * H, 128, 5 * 896))


def _consts():
    p = np.arange(128, dtype=np.float32)
    pos = np.stack([p + 1, 127 - p, p, 128 - p], 1)
    jj, ii = np.meshgrid(p, p, indexing="ij")
    D1 = np.maximum(ii - jj, 0)
    D2 = np.maximum(jj - ii, 0)
    return np.ascontiguousarray(np.concatenate([pos, D1, D2, np.eye(128, dtype=np.float32)], 1).astype(np.float32))


def make_in_maps(kb, inputs, batches):
    dep = kb.depth
    x, c, ctx, c_ctx = inputs["x"], inputs["c"], inputs["ctx"], inputs["c_ctx"]
    shared = {
        "adab": _col(np.asarray(inputs["ada_b"], np.float32).reshape(dep, 9, D)).reshape(128, dep * 144),
        "ng": _col(inputs["norm_g"]).reshape(128, dep * 48),
        "fg": _col(inputs["final_g"]).reshape(128, 16),
        "lgd": np.ascontiguousarray(np.broadcast_to(np.asarray(inputs["ret_log_decay"], np.float32).reshape(1, -1), (128, kb.nret * 16))),
        "cst": _consts(),
        "rope": _rope_tables(kb.nlat),
        "ada_w": np.asarray(inputs["ada_w"], np.float32),
        "ffn_w_gu": np.asarray(inputs["ffn_w_gu"], np.float32),
        "ffn_w_down": np.asarray(inputs["ffn_w_down"], np.float32),
        "ret_w_in": np.asarray(inputs["ret_w_in"], np.float32),
        "ret_w_out": np.asarray(inputs["ret_w_out"], np.float32),
        "na_w_in": np.asarray(inputs["na_w_in"], np.float32),
        "na_w_out": np.asarray(inputs["na_w_out"], np.float32),
    }
    lg = np.asarray(inputs["ret_log_decay"], np.float32).reshape(kb.nret, 16)
    lgp = np.zeros((kb.nret, 32), np.float32)
    lgp[:, :16] = lg
    shared["lgd"] = np.ascontiguousarray(np.broadcast_to(lgp.reshape(1, -1), (128, kb.nret * 32)))
    if kb.nna > 0:
        shared["nab"] = _na_bias(np.asarray(inputs["na_rpb"], np.float32), kb.rows)
    else:
        shared["nab"] = np.zeros((NAH, 128, 5 * 896), np.float32)
    maps = []
    for b in batches:
        m = dict(shared)
        m["xin"] = np.ascontiguousarray(np.concatenate([np.asarray(x[b], np.float32).T, np.asarray(ctx[b], np.float32).T], 1))
        m["cc"] = np.ascontiguousarray(np.concatenate([_col(c[b]), _col(c_ctx)], 1))
        maps.append(m)
    return maps


_CACHE = {}


def kernel(**inputs):
    x = np.asarray(inputs["x"])
    B, nlat = x.shape[0], x.shape[1]
    dep = np.asarray(inputs["ada_w"]).shape[0]
    key = (nlat, dep)
    if key not in _CACHE:
        kb = K(nlat, dep)
        _CACHE[key] = (kb, kb.build())
    kb, nc = _CACHE[key]
    n_cores = B
    batches = list(range(B))
    maps = make_in_maps(kb, inputs, batches)
    res = run_bass_kernel_spmd(nc, maps, core_ids=list(range(n_cores)))
    out = np.stack([np.ascontiguousarray(res.results[b]["out"].T) for b in range(B)], 0)
    return out.astype(np.float32)
```

```python
import contextlib
import numpy as np
import concourse.bass as bass
import concourse.mybir as mybir
from concourse.bass_utils import run_bass_kernel_spmd

F32 = mybir.dt.float32
BF16 = mybir.dt.bfloat16
AF = mybir.ActivationFunctionType
ALU = mybir.AluOpType
AX = mybir.AxisListType

D = 2048
NCH = 16
DFF = 5504
NF = 43
EPS = 1e-6
GRID_W = 64
CTX = 256
RH, RDK, RDV = 8, 256, 512
NAH, NADH = 16, 128
NA_KR, NA_KC = 8, 16
ENGS = ["pe", "act", "dve", "pool", "sp"]


class Res:
    __slots__ = ("w", "rd")

    def __init__(self):
        self.w = None
        self.rd = []


class Op:
    __slots__ = ("eng", "fn", "deps", "dma", "tok", "need_tok", "bg", "idx")

    def __init__(self, eng, fn, dma, bg):
        self.eng = eng
        self.fn = fn
        self.dma = dma
        self.bg = bg
        self.deps = []
        self.tok = None
        self.need_tok = False


class Prog:
    def __init__(self, nc):
        self.nc = nc
        self.ops = []
        self.streams = {e: [] for e in ENGS}
        self.ndma_sems = {"sp": 24, "act": 4, "pool": 12}
        self.dma_hist = {e: [] for e in ENGS}
        self.last = {e: None for e in ENGS}
        self.since_fence = []
        self.fence_deps = {e: [] for e in ENGS}

    def op(self, eng, fn, reads=(), writes=(), dma=False, bg=False):
        o = Op(eng, fn, dma, bg)
        o.idx = len(self.ops)
        deps = {}

        def add(d):
            key = id(d) if d.dma else d.eng
            p = deps.get(key)
            if p is None or p.idx < d.idx:
                deps[key] = d

        for r in reads:
            if r.w is not None:
                add(r.w)
        for w in writes:
            if w.w is not None:
                add(w.w)
            for x in w.rd:
                add(x)
        if dma:
            h = self.dma_hist[eng]
            k = self.ndma_sems[eng]
            if len(h) >= k:
                add(h[len(h) - k])
            h.append(o)
            if not bg:
                self.since_fence.append(o)
        if self.fence_deps[eng] and not bg:
            for p in self.fence_deps[eng]:
                add(p)
            self.fence_deps[eng] = []
        for d in deps.values():
            if (not d.dma) and (not dma) and d.eng == eng and eng == "pe":
                continue
            d.need_tok = True
            o.deps.append(d)
        for w in writes:
            w.w = o
            w.rd = []
        for r in reads:
            if dma:
                r.rd.append(o)
            else:
                r.rd = [x for x in r.rd if x.dma or x.eng != eng]
                r.rd.append(o)
        self.ops.append(o)
        self.streams[eng].append(o)
        if not dma and not bg:
            self.last[eng] = o
        return o

    def fence(self):
        deps = [o for o in self.last.values() if o is not None]
        deps += [o for o in self.since_fence if not o.need_tok]
        self.since_fence = []
        for e in ENGS:
            self.fence_deps[e] = list(deps)

    def emit(self, final_waits=()):
        nc = self.nc
        for o in final_waits:
            o.need_tok = True
        with contextlib.ExitStack() as st:
            esem = {e: st.enter_context(nc.semaphore("s_" + e)) for e in ["pe", "act", "dve", "pool"]}
            dsem = {e: [st.enter_context(nc.semaphore("d_%s%d" % (e, i))) for i in range(n)]
                    for e, n in self.ndma_sems.items()}
            cnt = {e: 0 for e in esem}
            dcnt = {e: [0] * n for e, n in self.ndma_sems.items()}
            dk = {e: 0 for e in self.ndma_sems}
            for o in self.ops:
                if o.dma:
                    e = o.eng
                    k = dk[e] % self.ndma_sems[e]
                    dk[e] += 1
                    dcnt[e][k] += 16
                    o.tok = (dsem[e][k], dcnt[e][k])
                elif o.need_tok:
                    cnt[o.eng] += 1
                    o.tok = (esem[o.eng], cnt[o.eng])
            block = st.enter_context(nc.Block())

            def run(ename):
                def body(eng):
                    known = {}
                    for o in self.streams[ename]:
                        for d in o.deps:
                            sem, val = d.tok
                            if known.get(id(sem), 0) >= val:
                                continue
                            known[id(sem)] = val
                            eng.wait_ge(sem, val)
                        ins = o.fn(eng)
                        if o.tok is not None:
                            ins.then_inc(o.tok[0], 16 if o.dma else 1)
                    if ename == "sp":
                        for o in final_waits:
                            sem, val = o.tok
                            if known.get(id(sem), 0) >= val:
                                continue
                            known[id(sem)] = val
                            eng.wait_ge(sem, val)
                return body

            block.tensor(run("pe"))
            block.scalar(run("act"))
            block.vector(run("dve"))
            block.gpsimd(run("pool"))
            block.sync(run("sp"))


class Arena:
    def __init__(self, nc, words):
        self.t = nc.alloc_sbuf_tensor("arena", [128, words], F32)
        self.words = words
        self.off = 0

    def alloc(self, shape, dtype):
        n = int(np.prod(shape))
        w = n if dtype == F32 else (n + 1) // 2
        w = (w + 7) // 8 * 8
        assert self.off + w <= self.words, ("arena overflow", self.off, w, self.words)
        v = self.t[:, self.off:self.off + w]
        self.off += w
        if dtype != F32:
            v = v.bitcast(dtype)
        v = v[:, :n]
        if len(shape) == 2:
            v = v.rearrange("p (a b) -> p a b", b=shape[1])
        elif len(shape) == 3:
            v = v.rearrange("p (a b c) -> p a b c", b=shape[1], c=shape[2])
        return v

    def mark(self):
        return self.off

    def reset(self, m):
        self.off = m


class K:
    def __init__(self, nlat=4096, depth=4):
        self.nlat = nlat
        self.depth = depth
        self.rows = nlat // GRID_W
        self.nlt = nlat // 128
        self.nt = self.nlt + 2
        self.ntok = nlat + CTX
        self.blocks = [(i * 512, 512) for i in range(nlat // 512)] + [(nlat, CTX)]
        self.mixers = ["ret" if i % 2 == 0 else "na" for i in range(depth)]
        self.nret = (depth + 1) // 2
        self.nna = depth // 2

    def build(self):
        nc = bass.Bass("TRN2", target_bir_lowering=False)
        self.nc = nc
        P = self.P = Prog(nc)
        dep, ntok = self.depth, self.ntok
        ein = lambda n, s, dt=F32: nc.dram_tensor(n, s, dt, kind="ExternalInput").ap()
        scr = lambda n, s, dt=BF16: nc.dram_tensor(n, s, dt).ap()
        self.xin = ein("xin", [D, ntok])
        self.out = nc.dram_tensor("out", [D, self.nlat], F32, kind="ExternalOutput").ap()
        self.cc_in = ein("cc", [128, 32])
        self.adab_in = ein("adab", [128, dep * 144])
        self.ng_in = ein("ng", [128, dep * 48])
        self.fg_in = ein("fg", [128, 16])
        self.lgd_in = ein("lgd", [128, self.nret * 32])
        self.cst_in = ein("cst", [128, 4 + 3 * 128])
        self.rope_in = ein("rope", [self.nlt, 128, 512])
        self.nab_in = ein("nab", [max(self.nna, 1) * NAH, 128, 5 * 640])
        self.w_ada = ein("ada_w", [dep, D, 9 * D])
        self.w_gu = ein("ffn_w_gu", [dep, 2, D, 2 * DFF])
        self.w_d = ein("ffn_w_down", [dep, 2, DFF, D])
        self.w_rin = ein("ret_w_in", [self.nret, D, 12288])
        self.w_rout = ein("ret_w_out", [self.nret, 4096, D])
        self.w_nin = ein("na_w_in", [max(self.nna, 1), D, 3 * D])
        self.w_nout = ein("na_w_out", [max(self.nna, 1), D, D])
        nn_ = max(self.nna, 1)
        self.b_ada = [scr("b_ada%d" % l, [D, 9 * D]) for l in range(dep)]
        self.b_gu = {(l, f): scr("b_gu%d_%d" % (l, f), [D, 2 * DFF]) for l in range(dep) for f in range(2)}
        self.b_d = {(l, f): scr("b_d%d_%d" % (l, f), [DFF, D]) for l in range(dep) for f in range(2)}
        self.b_rin = [scr("b_rin%d" % j, [D, 12288]) for j in range(self.nret)]
        self.b_rout = [scr("b_rout%d" % j, [4096, D]) for j in range(self.nret)]
        self.b_nin = [scr("b_nin%d" % j, [D, 3 * D]) for j in range(nn_)]
        self.b_nout = [scr("b_nout%d" % j, [D, D]) for j in range(nn_)]
        self.wres = {}
        self.XT = scr("XT", [D, ntok], F32)
        self.XT_r = [Res() for _ in self.blocks]
        self.QKVG = scr("QKVG", [ntok, 12288])
        self.QKVG_r = Res()
        self.OGT = scr("OGT", [4096, ntok])
        self.OGT_r = Res()

        A = self.A = Arena(nc, 50000)
        self.ps = [nc.alloc_psum_tensor("ps%d" % i, [128, 512], F32) for i in range(8)]
        self.ps_r = [Res() for _ in range(8)]
        self.c_r = Res()
        self.ones = A.alloc([128], BF16)
        self.ident = A.alloc([128], BF16)
        self.epsc = A.alloc([1], F32)
        self.cst = A.alloc([4 + 3 * 128], F32)
        self.mods = A.alloc([dep * 2 * 9 * 16], F32)
        self.mods_r = Res()
        self.fg = A.alloc([16], F32)
        P.op("dve", lambda e: e.memset(self.ones, 1.0 / D), writes=[self.c_r])
        P.op("dve", lambda e: e.memset(self.epsc, EPS), writes=[self.c_r])
        P.op("sp", lambda e: e.dma_start(out=self.cst, in_=self.cst_in), writes=[self.c_r], dma=True)
        P.op("sp", lambda e: e.dma_start(out=self.fg, in_=self.fg_in), writes=[self.c_r], dma=True)
        P.op("dve", lambda e: e.tensor_copy(out=self.ident, in_=self.cst[:, 4 + 256:4 + 384]),
             reads=[self.c_r], writes=[self.c_r])
        self.base_mark = A.mark()

        for bi, (t0, tb) in enumerate(self.blocks):
            P.op("sp", lambda e, t0=t0, tb=tb: e.dma_start(out=self.XT[:, t0:t0 + tb], in_=self.xin[:, t0:t0 + tb]),
                 writes=[self.XT_r[bi]], dma=True)

        self.cast_all()
        self.mod_setup()
        self.base_mark = A.mark()
        self.mod_phase(0)
        P.fence()
        outs = []
        for l in range(dep):
            self.stage_blocks(l)
            P.fence()
            m_pre = A.mark()
            if l + 1 < dep:
                self.mod_phase(l + 1, reset=False)
            if self.mixers[l] == "ret":
                self.ret_core(l)
            else:
                self.na_core(l)
            A.reset(m_pre)
            P.fence()
        outs = self.stage_blocks(dep)
        P.emit(final_waits=outs)
        return nc

    def cast(self, key, src, dst, rows, rstep=256):
        P = self.P
        rl = []
        for r0 in range(0, rows, rstep):
            r1 = min(rows, r0 + rstep)
            r = Res()
            P.op("pool", lambda e, r0=r0, r1=r1: e.dma_start(out=dst[r0:r1, :], in_=src[r0:r1, :], max_dma_last_dim=4096),
                 writes=[r], dma=True, bg=True)
            rl.append(r)
        self.wres[key] = rl

    def cast_all(self):
        for l in range(self.depth):
            j = l // 2
            self.cast(("ada", l), self.w_ada[l], self.b_ada[l], D)
            self.cast(("gu", l, 0), self.w_gu[l, 0], self.b_gu[(l, 0)], D)
            self.cast(("d", l, 0), self.w_d[l, 0], self.b_d[(l, 0)], DFF)
            if self.mixers[l] == "ret":
                self.cast(("min", l), self.w_rin[j], self.b_rin[j], D)
                self.cast(("mout", l), self.w_rout[j], self.b_rout[j], 4096)
            else:
                self.cast(("min", l), self.w_nin[j], self.b_nin[j], D)
                self.cast(("mout", l), self.w_nout[j], self.b_nout[j], D)
            self.cast(("gu", l, 1), self.w_gu[l, 1], self.b_gu[(l, 1)], D)
            self.cast(("d", l, 1), self.w_d[l, 1], self.b_d[(l, 1)], DFF)

    def modcol(self, l, who, k, which):
        o = (((l * 2 + who) * 3 + k) * 3 + which) * 16
        return self.mods[:, o:o + 16]

    def mod_setup(self):
        P, A = self.P, self.A
        dep = self.depth
        self.m_cc = A.alloc([32], F32)
        self.m_sc = A.alloc([16, 2], BF16)
        self.m_sg = A.alloc([32], F32)
        self.m_adab = A.alloc([dep * 144], F32)
        self.m_ng = A.alloc([dep * 48], F32)
        self.m_r = Res()
        cc, sc, sg, r = self.m_cc, self.m_sc, self.m_sg, self.m_r
        P.op("sp", lambda e: e.dma_start(out=cc, in_=self.cc_in), writes=[r], dma=True)
        P.op("sp", lambda e: e.dma_start(out=self.m_adab, in_=self.adab_in), writes=[r], dma=True)
        P.op("sp", lambda e: e.dma_start(out=self.m_ng, in_=self.ng_in), writes=[r], dma=True)
        P.op("act", lambda e: e.activation(out=sg, in_=cc, func=AF.Sigmoid), reads=[r], writes=[r])
        P.op("dve", lambda e: e.tensor_tensor(out=sc[:, :, 0], in0=cc[:, 0:16], in1=sg[:, 0:16], op=ALU.mult), reads=[r], writes=[r])
        P.op("dve", lambda e: e.tensor_tensor(out=sc[:, :, 1], in0=cc[:, 16:32], in1=sg[:, 16:32], op=ALU.mult), reads=[r], writes=[r])

    def mod_phase(self, l0, reset=True):
        P, A = self.P, self.A
        m0 = A.mark()
        sc, adab, ng, r = self.m_sc, self.m_adab, self.m_ng, self.m_r
        raw = A.alloc([2, 144], F32)
        wt = [A.alloc([NCH, 256], BF16) for _ in range(3)]
        wt_r = [Res() for _ in range(3)]
        wi = 0
        for l in [l0]:
            ps = self.ps[7]
            ps_r = self.ps_r[7]
            psv = ps[:, 0:288].rearrange("p (a b) -> p a b", b=2)
            for g in range(72):
                w, w_r = wt[wi % 3], wt_r[wi % 3]
                wi += 1
                src = self.b_ada[l][:, g * 256:(g + 1) * 256].rearrange("(c p) f -> p c f", p=128)
                P.op("sp", lambda e, w=w, src=src: e.dma_start(out=w, in_=src), reads=self.wres[("ada", l)], writes=[w_r], dma=True)
                for s in range(2):
                    cidx = g * 2 + s
                    for c in range(NCH):
                        P.op("pe", lambda e, w=w, s=s, c=c, cidx=cidx, psv=psv: e.matmul(
                            psv[:, cidx, :], lhsT=w[:, c, s * 128:(s + 1) * 128], rhs=sc[:, c, :],
                            start=(c == 0), stop=(c == NCH - 1)), reads=[w_r, r], writes=[ps_r])
            for who in range(2):
                P.op("dve", lambda e, who=who, psv=psv, l=l: e.tensor_tensor(
                    out=raw[:, who, :], in0=psv[:, :, who], in1=adab[:, l * 144:(l + 1) * 144], op=ALU.add),
                    reads=[ps_r, r], writes=[r])
                for k in range(3):
                    sh = raw[:, who, (3 * k) * 16:(3 * k + 1) * 16]
                    scl = raw[:, who, (3 * k + 1) * 16:(3 * k + 2) * 16]
                    gt = raw[:, who, (3 * k + 2) * 16:(3 * k + 3) * 16]
                    g_ = ng[:, (l * 3 + k) * 16:(l * 3 + k + 1) * 16]
                    P.op("dve", lambda e, l=l, who=who, k=k, scl=scl, g_=g_: e.scalar_tensor_tensor(
                        out=self.modcol(l, who, k, 0), in0=scl, scalar=1.0, in1=g_, op0=ALU.add, op1=ALU.mult),
                        reads=[r], writes=[self.mods_r])
                    P.op("dve", lambda e, l=l, who=who, k=k, sh=sh: e.tensor_copy(out=self.modcol(l, who, k, 1), in_=sh),
                         reads=[r], writes=[self.mods_r])
                    P.op("dve", lambda e, l=l, who=who, k=k, gt=gt: e.tensor_scalar(
                        out=self.modcol(l, who, k, 2), in0=gt, scalar1=(1.0 if k == 1 else 0.5), scalar2=None, op0=ALU.mult),
                        reads=[r], writes=[self.mods_r])
        if reset:
            A.reset(m0)

    def stage_blocks(self, l):
        P, A = self.P, self.A
        m0 = A.mark()
        S = type("S", (), {})()
        S.xT = A.alloc([NCH, 512], F32); S.xT_r = Res()
        S.hT = A.alloc([NCH, 512], BF16); S.hT_r = Res()
        S.aT = A.alloc([NF, 512], BF16); S.aT_r = [Res() for _ in range(NF)]
        S.rstd = A.alloc([512], F32); S.rstd_r = Res()
        S.tmp = [A.alloc([512], F32) for _ in range(3)]; S.tmp_r = [Res() for _ in range(3)]; S.ti = 0
        S.wgu = [(A.alloc([NCH, 256], BF16), Res()) for _ in range(4)]; S.wgi = 0
        S.wd = [(A.alloc([8, 512], BF16), Res()) for _ in range(3)]; S.wdi = 0
        S.rope = [(A.alloc([512], F32), Res()) for _ in range(4)]
        S.stg = [(A.alloc([256], BF16), Res()) for _ in range(4)]; S.si = 0
        S.ev = 0
        outs = []
        for bi, (t0, tb) in enumerate(self.blocks):
            who = 0 if t0 < self.nlat else 1
            src = self.XT[:, t0:t0 + tb].rearrange("(c p) t -> p c t", p=128)
            P.op("sp", lambda e, src=src, tb=tb: e.dma_start(out=S.xT[:, :, :tb], in_=src),
                 reads=[self.XT_r[bi]], writes=[S.xT_r], dma=True)
            if l > 0:
                pl = l - 1
                last = (pl == self.depth - 1)
                if not (last and who == 1):
                    kc = 32 if self.mixers[pl] == "ret" else 16
                    srco = self.OGT[0:kc * 128, t0:t0 + tb].rearrange("(c p) t -> p c t", p=128)
                    P.op("sp", lambda e, srco=srco, tb=tb, kc=kc: e.dma_start(out=S.aT[:, :kc, :tb], in_=srco),
                         reads=[self.OGT_r], writes=S.aT_r[:kc], dma=True)
                    wout = self.b_rout[pl // 2] if self.mixers[pl] == "ret" else self.b_nout[pl // 2]
                    self.down_proj(S, tb, wout, self.wres[("mout", pl)], kc, self.modcol(pl, who, 1, 2))
                    self.adanorm(S, tb, self.modcol(pl, who, 2, 0), self.modcol(pl, who, 2, 1))
                    self.ffn(S, tb, pl, 1, self.modcol(pl, who, 2, 2))
            if l < self.depth:
                self.adanorm(S, tb, self.modcol(l, who, 0, 0), self.modcol(l, who, 0, 1))
                self.ffn(S, tb, l, 0, self.modcol(l, who, 0, 2))
                self.adanorm(S, tb, self.modcol(l, who, 1, 0), self.modcol(l, who, 1, 1))
                self.in_proj(S, l, t0, tb, who)
                dst = self.XT[:, t0:t0 + tb].rearrange("(c p) t -> p c t", p=128)
                P.op("sp", lambda e, dst=dst, tb=tb: e.dma_start(out=dst, in_=S.xT[:, :, :tb]),
                     reads=[S.xT_r], writes=[self.XT_r[bi]], dma=True)
            elif who == 0:
                self.adanorm(S, tb, self.fg, None, final=True)
                dst = self.out[:, t0:t0 + tb].rearrange("(c p) t -> p c t", p=128)
                outs.append(P.op("sp", lambda e, dst=dst, tb=tb: e.dma_start(out=dst, in_=S.xT[:, :, :tb]),
                                 reads=[S.xT_r], writes=[Res()], dma=True))
        A.reset(m0)
        return outs

    def ntmp(self, S):
        i = S.ti % 3
        S.ti += 1
        return S.tmp[i], S.tmp_r[i]

    def adanorm(self, S, tb, Gc, Sc, final=False):
        P = self.P
        P.op("act", lambda e: e.activation(out=S.hT[:, :, :tb], in_=S.xT[:, :, :tb], func=AF.Square),
             reads=[S.xT_r], writes=[S.hT_r])
        ps, ps_r = self.ps[0], self.ps_r[0]
        for c in range(NCH):
            P.op("pe", lambda e, c=c: e.matmul(ps[:, :tb], lhsT=self.ones, rhs=S.hT[:, c, :tb],
                                                start=(c == 0), stop=(c == NCH - 1)),
                 reads=[self.c_r, S.hT_r], writes=[ps_r])
        P.op("act", lambda e: e.activation(out=S.rstd[:, :tb], in_=ps[:, :tb], func=AF.Sqrt, bias=self.epsc[:, 0:1]),
             reads=[ps_r, self.c_r], writes=[S.rstd_r])
        P.op("dve", lambda e: e.reciprocal(out=S.rstd[:, :tb], in_=S.rstd[:, :tb]), reads=[S.rstd_r], writes=[S.rstd_r])
        for c in range(NCH):
            if final:
                P.op("dve", lambda e, c=c: e.scalar_tensor_tensor(
                    out=S.xT[:, c, :tb], in0=S.xT[:, c, :tb], scalar=Gc[:, c:c + 1], in1=S.rstd[:, :tb],
                    op0=ALU.mult, op1=ALU.mult), reads=[S.xT_r, S.rstd_r, self.c_r], writes=[S.xT_r])
                continue
            tmp, tmp_r = self.ntmp(S)
            P.op("dve", lambda e, c=c, tmp=tmp: e.scalar_tensor_tensor(
                out=tmp[:, :tb], in0=S.xT[:, c, :tb], scalar=Gc[:, c:c + 1], in1=S.rstd[:, :tb],
                op0=ALU.mult, op1=ALU.mult), reads=[S.xT_r, S.rstd_r, self.mods_r], writes=[tmp_r])
            P.op("act", lambda e, c=c, tmp=tmp: e.activation(
                out=S.hT[:, c, :tb], in_=tmp[:, :tb], func=AF.Identity, bias=Sc[:, c:c + 1]),
                reads=[tmp_r, self.mods_r], writes=[S.hT_r])

    def ffn(self, S, tb, l, f, HGc):
        P = self.P
        Wgu, Wd = self.b_gu[(l, f)], self.b_d[(l, f)]
        gu_r, d_r = self.wres[("gu", l, f)], self.wres[("d", l, f)]
        bank = 0
        for g in range((NF + 1) // 2):
            nf = min(2, NF - g * 2)
            ncol = nf * 128
            wg, wg_r = S.wgu[S.wgi % 4]; S.wgi += 1
            wu, wu_r = S.wgu[S.wgi % 4]; S.wgi += 1
            c0 = g * 256
            srcg = Wgu[:, c0:c0 + ncol].rearrange("(c p) f -> p c f", p=128)
            srcu = Wgu[:, DFF + c0:DFF + c0 + ncol].rearrange("(c p) f -> p c f", p=128)
            P.op("sp", lambda e, wg=wg, srcg=srcg, ncol=ncol: e.dma_start(out=wg[:, :, :ncol], in_=srcg),
                 reads=gu_r, writes=[wg_r], dma=True)
            P.op("sp", lambda e, wu=wu, srcu=srcu, ncol=ncol: e.dma_start(out=wu[:, :, :ncol], in_=srcu),
                 reads=gu_r, writes=[wu_r], dma=True)
            for fl in range(nf):
                fi = g * 2 + fl
                bg = 1 + bank % 3; bu = 1 + (bank + 1) % 3; bank += 2
                if bg == bu:
                    bu = 1 + (bank) % 3
                for c in range(NCH):
                    P.op("pe", lambda e, c=c, fl=fl, wg=wg, bg=bg: e.matmul(
                        self.ps[bg][:, :tb], lhsT=wg[:, c, fl * 128:(fl + 1) * 128], rhs=S.hT[:, c, :tb],
                        start=(c == 0), stop=(c == NCH - 1)), reads=[wg_r, S.hT_r], writes=[self.ps_r[bg]])
                for c in range(NCH):
                    P.op("pe", lambda e, c=c, fl=fl, wu=wu, bu=bu: e.matmul(
                        self.ps[bu][:, :tb], lhsT=wu[:, c, fl * 128:(fl + 1) * 128], rhs=S.hT[:, c, :tb],
                        start=(c == 0), stop=(c == NCH - 1)), reads=[wu_r, S.hT_r], writes=[self.ps_r[bu]])
                tmp, tmp_r = self.ntmp(S)
                P.op("act", lambda e, tmp=tmp, bg=bg: e.activation(out=tmp[:, :tb], in_=self.ps[bg][:, :tb], func=AF.Silu),
                     reads=[self.ps_r[bg]], writes=[tmp_r])
                P.op("dve", lambda e, tmp=tmp, bu=bu, fi=fi: e.tensor_tensor(
                    out=S.aT[:, fi, :tb], in0=tmp[:, :tb], in1=self.ps[bu][:, :tb], op=ALU.mult),
                    reads=[tmp_r, self.ps_r[bu]], writes=[S.aT_r[fi]])
        self.down_proj(S, tb, Wd, d_r, NF, HGc)

    def down_proj(self, S, tb, W, W_r, kc, HGc):
        P = self.P
        for q in range(4):
            for g in range((kc + 7) // 8):
                nf = min(8, kc - g * 8)
                wd, wd_r = S.wd[S.wdi % 3]; S.wdi += 1
                src = W[g * 1024:(g * 8 + nf) * 128, q * 512:(q + 1) * 512].rearrange("(f p) n -> p f n", p=128)
                P.op("act", lambda e, wd=wd, src=src, nf=nf: e.dma_start(out=wd[:, :nf, :], in_=src),
                     reads=W_r, writes=[wd_r], dma=True)
                for fl in range(nf):
                    fi = g * 8 + fl
                    for j in range(4):
                        P.op("pe", lambda e, wd=wd, fl=fl, j=j, fi=fi: e.matmul(
                            self.ps[4 + j][:, :tb], lhsT=wd[:, fl, j * 128:(j + 1) * 128], rhs=S.aT[:, fi, :tb],
                            start=(fi == 0), stop=(fi == kc - 1)), reads=[wd_r, S.aT_r[fi]], writes=[self.ps_r[4 + j]])
            for j in range(4):
                c = q * 4 + j
                P.op("dve", lambda e, j=j, c=c: e.scalar_tensor_tensor(
                    out=S.xT[:, c, :tb], in0=self.ps[4 + j][:, :tb], scalar=HGc[:, c:c + 1], in1=S.xT[:, c, :tb],
                    op0=ALU.mult, op1=ALU.add), reads=[self.ps_r[4 + j], S.xT_r, self.mods_r], writes=[S.xT_r])

    def in_proj(self, S, l, t0, tb, who):
        P = self.P
        ret = self.mixers[l] == "ret"
        W = self.b_rin[l // 2] if ret else self.b_nin[l // 2]
        W_r = self.wres[("min", l)]
        ncols = 12288 if ret else 3 * D
        ntile = tb // 128
        if ret and who == 0:
            for ti in range(ntile):
                rp, rp_r = S.rope[ti]
                P.op("sp", lambda e, rp=rp, ti=ti: e.dma_start(out=rp, in_=self.rope_in[t0 // 128 + ti]),
                     writes=[rp_r], dma=True)
        for g in range(ncols // 256):
            w, w_r = S.wgu[S.wgi % 4]; S.wgi += 1
            src = W[:, g * 256:(g + 1) * 256].rearrange("(c p) f -> p c f", p=128)
            P.op("sp", lambda e, w=w, src=src: e.dma_start(out=w, in_=src), reads=W_r, writes=[w_r], dma=True)
            col0 = g * 256
            kind = "copy"
            if ret and col0 < 4096:
                kind = "q" if col0 < 2048 else "k"
            elif ret and col0 >= 8192:
                kind = "silu"
            for ti in range(ntile):
                b = 1 + (S.ev % 3)
                ps, ps_r = self.ps[b], self.ps_r[b]
                for c in range(NCH):
                    P.op("pe", lambda e, c=c, ti=ti, w=w, ps=ps: e.matmul(
                        ps[:, :256], lhsT=S.hT[:, c, ti * 128:(ti + 1) * 128], rhs=w[:, c, :],
                        start=(c == 0), stop=(c == NCH - 1)), reads=[w_r, S.hT_r], writes=[ps_r])
                stg, stg_r = S.stg[S.si % 4]; S.si += 1
                if kind == "silu":
                    P.op("act", lambda e, stg=stg, ps=ps: e.activation(out=stg, in_=ps[:, :256], func=AF.Silu),
                         reads=[ps_r], writes=[stg_r])
                elif kind == "copy" or who == 1:
                    sc = 0.0625 if kind == "k" else 1.0
                    if S.ev % 2 == 0:
                        P.op("act", lambda e, stg=stg, ps=ps, sc=sc: e.activation(out=stg, in_=ps[:, :256], func=AF.Copy, scale=sc),
                             reads=[ps_r], writes=[stg_r])
                    else:
                        P.op("dve", lambda e, stg=stg, ps=ps, sc=sc: e.tensor_scalar(
                            out=stg, in0=ps[:, :256], scalar1=sc, scalar2=None, op0=ALU.mult), reads=[ps_r], writes=[stg_r])
                else:
                    sc = 0.0625 if kind == "k" else 1.0
                    rp, rp_r = S.rope[ti]
                    t1, t1_r = self.ntmp(S)
                    t2, t2_r = self.ntmp(S)
                    P.op("dve", lambda e, t1=t1, ps=ps, rp=rp: e.tensor_tensor(
                        out=t1[:, 0:256], in0=ps[:, :256], in1=rp[:, 0:256], op=ALU.mult), reads=[ps_r, rp_r], writes=[t1_r])
                    psv = ps[:, :256].rearrange("p (a h x) -> p a h x", a=2, h=2)
                    t2v = t2[:, 0:256].rearrange("p (a h x) -> p a h x", a=2, h=2)
                    rpv = rp[:, 256:512].rearrange("p (a h x) -> p a h x", a=2, h=2)
                    P.op("dve", lambda e, t2v=t2v, psv=psv, rpv=rpv: e.tensor_tensor(
                        out=t2v[:, :, 0, :], in0=psv[:, :, 1, :], in1=rpv[:, :, 0, :], op=ALU.mult), reads=[ps_r, rp_r], writes=[t2_r])
                    P.op("dve", lambda e, t2v=t2v, psv=psv, rpv=rpv: e.tensor_tensor(
                        out=t2v[:, :, 1, :], in0=psv[:, :, 0, :], in1=rpv[:, :, 1, :], op=ALU.mult), reads=[ps_r, rp_r], writes=[t2_r])
                    P.op("dve", lambda e, stg=stg, t1=t1, t2=t2, sc=sc: e.scalar_tensor_tensor(
                        out=stg, in0=t1[:, 0:256], scalar=sc, in1=t2[:, 0:256], op0=ALU.mult, op1=ALU.add) if sc == 1.0 else
                        e.tensor_tensor(out=stg, in0=t1[:, 0:256], in1=t2[:, 0:256], op=ALU.add),
                        reads=[t1_r, t2_r], writes=[stg_r])
                    if sc != 1.0:
                        P.op("act", lambda e, stg=stg, sc=sc: e.activation(out=stg, in_=stg, func=AF.Copy, scale=sc),
                             reads=[stg_r], writes=[stg_r])
                S.ev += 1
                tok0 = t0 + ti * 128
                P.op("sp", lambda e, stg=stg, tok0=tok0, col0=col0: e.dma_start(
                    out=self.QKVG[tok0:tok0 + 128, col0:col0 + 256], in_=stg), reads=[stg_r], writes=[Res()], dma=True)

    def transpose_to(self, dst, dst_r, src, src_r, n, evac):
        P = self.P
        b = self.tb_bank
        self.tb_bank = 1 - self.tb_bank
        psb = self.ps[b][:, :].bitcast(BF16)
        for i in range(n):
            P.op("pe", lambda e, i=i, psb=psb: e.transpose(out=psb[:, i * 128:(i + 1) * 128], in_=src[:, i * 128:(i + 1) * 128],
                                                          identity=self.ident), reads=[src_r, self.c_r], writes=[self.ps_r[b]])
        if evac == "act":
            P.op("act", lambda e, psb=psb: e.activation(out=dst, in_=psb[:, :n * 128], func=AF.Copy), reads=[self.ps_r[b]], writes=[dst_r])
        else:
            P.op("dve", lambda e, psb=psb: e.tensor_copy(out=dst, in_=psb[:, :n * 128]), reads=[self.ps_r[b]], writes=[dst_r])

    def ret_core(self, l):
        P, A = self.P, self.A
        m0 = A.mark()
        self.tb_bank = 0
        j = l // 2
        nt, nlt = self.nt, self.nlt
        need_ctx = l != self.depth - 1
        r = Res()
        lgd = A.alloc([32], F32)
        LG = A.alloc([16], F32)
        dec = A.alloc([4, 8], F32)
        cdec = A.alloc([2, 8], F32)
        MT = A.alloc([8, 128], F32)
        t128 = A.alloc([128], F32)
        pos = self.cst[:, 0:4]
        D1 = self.cst[:, 4:132]; D2 = self.cst[:, 132:260]; I_ = self.cst[:, 260:388]
        P.op("sp", lambda e: e.dma_start(out=lgd, in_=self.lgd_in[:, j * 32:(j + 1) * 32]), writes=[r], dma=True)
        P.op("act", lambda e: e.activation(out=LG, in_=lgd[:, 0:16], func=AF.Exp), reads=[r], writes=[r])
        P.op("dve", lambda e: e.tensor_scalar(out=LG, in0=LG, scalar1=-1.0, scalar2=None, op0=ALU.mult), reads=[r], writes=[r])
        for k, (pc, off) in enumerate([(0, 0), (1, 0), (3, 8), (2, 8)]):
            P.op("dve", lambda e, k=k, pc=pc, off=off: e.tensor_scalar(
                out=dec[:, k, :], in0=LG[:, off:off + 8], scalar1=pos[:, pc:pc + 1], scalar2=None, op0=ALU.mult),
                reads=[r, self.c_r], writes=[r])
        P.op("act", lambda e: e.activation(out=dec, in_=dec, func=AF.Exp), reads=[r], writes=[r])
        P.op("act", lambda e: e.activation(out=cdec, in_=LG.rearrange("p (a b) -> p a b", b=8), func=AF.Exp, scale=128.0), reads=[r], writes=[r])
        for h in range(8):
            P.op("dve", lambda e, h=h: e.tensor_scalar(out=t128, in0=D1, scalar1=LG[:, h:h + 1], scalar2=None, op0=ALU.mult),
                 reads=[r, self.c_r], writes=[r])
            P.op("dve", lambda e, h=h: e.scalar_tensor_tensor(out=t128, in0=D2, scalar=LG[:, 8 + h:9 + h], in1=t128,
                                                               op0=ALU.mult, op1=ALU.add), reads=[r, self.c_r], writes=[r])
            P.op("act", lambda e: e.activation(out=t128, in_=t128, func=AF.Exp), reads=[r], writes=[r])
            P.op("dve", lambda e, h=h: e.tensor_tensor(out=MT[:, h, :], in0=t128, in1=I_, op=ALU.add), reads=[r, self.c_r], writes=[r])

        kk = A.alloc([nt, 256], BF16); kk_r = Res()
        vv = A.alloc([nt, 512], BF16); vv_r = Res()
        Sb = A.alloc([nlt + 1, 2, 512], BF16); Sb_r = [Res() for _ in range(nlt + 1)]
        Sm = A.alloc([2, 512], F32); Sm_r = Res()
        Sf = A.alloc([2, 512], BF16); Sf_r = Res()
        ring = lambda shape, dt, n: [(A.alloc(shape, dt), Res()) for _ in range(n)]
        qb = ring([256], BF16, 3); gb = ring([512], BF16, 3)
        kh = ring([256], BF16, 3)
        q3 = ring([3, 256], BF16, 2)
        qT = ring([3 * 256], BF16, 2)
        kT = ring([256], BF16, 2)
        sT = ring([128], BF16, 2)
        og = ring([512], BF16, 2)
        ogT = ring([512], BF16, 2)
        st = ring([8], F32, 2)
        onr = ring([512], F32, 2)
        it = {}

        def nxt(rg):
            k = it.get(id(rg), 0)
            it[id(rg)] = k + 1
            return rg[k % len(rg)]

        def state_update(c, h, kdidx, cidx):
            khb, khb_r = nxt(kh)
            P.op("dve", lambda e, khb=khb, c=c, h=h: e.tensor_scalar(
                out=khb, in0=kk[:, c, :], scalar1=dec[:, kdidx, h:h + 1], scalar2=None, op0=ALU.mult),
                reads=[kk_r, r], writes=[khb_r])
            for dc in range(2):
                P.op("pe", lambda e, dc=dc, khb=khb, c=c: e.matmul(
                    self.ps[2 + dc][:, :], lhsT=khb[:, dc * 128:(dc + 1) * 128], rhs=vv[:, c, :], start=True, stop=True),
                    reads=[khb_r, vv_r], writes=[self.ps_r[2 + dc]])
            for dc in range(2):
                P.op("dve", lambda e, dc=dc, h=h: e.scalar_tensor_tensor(
                    out=Sm[:, dc, :], in0=Sm[:, dc, :], scalar=cdec[:, cidx, h:h + 1], in1=self.ps[2 + dc][:, :],
                    op0=ALU.mult, op1=ALU.add), reads=[Sm_r, self.ps_r[2 + dc], r], writes=[Sm_r])

        for h in range(8):
            srck = self.QKVG[:, 2048 + h * 256:2048 + (h + 1) * 256].rearrange("(t p) f -> p t f", p=128)
            srcv = self.QKVG[:, 4096 + h * 512:4096 + (h + 1) * 512].rearrange("(t p) f -> p t f", p=128)
            P.op("sp", lambda e, srck=srck: e.dma_start(out=kk, in_=srck), reads=[self.QKVG_r], writes=[kk_r], dma=True)
            P.op("sp", lambda e, srcv=srcv: e.dma_start(out=vv, in_=srcv), reads=[self.QKVG_r], writes=[vv_r], dma=True)
            P.op("dve", lambda e: e.memset(Sm, 0.0), writes=[Sm_r])
            order = [nlt + 1, nlt] + list(range(nlt - 1, 0, -1))
            slot_of = {nlt: 0}
            for c in range(nlt):
                slot_of[c] = 1 + c
            for c in order:
                state_update(c, h, 3, 1)
                nxt_c = nlt if c == nlt + 1 else (nlt - 1 if c == nlt else c - 1)
                sl = slot_of[nxt_c]
                P.op("act", lambda e, sl=sl: e.activation(out=Sb[:, sl, :, :], in_=Sm, func=AF.Copy),
                     reads=[Sm_r], writes=[Sb_r[sl]])
            P.op("dve", lambda e: e.memset(Sm, 0.0), writes=[Sm_r])
            forder = [nlt, nlt + 1] + list(range(nlt))

            def stage_a(idx, c):
                is_ctx = c >= nlt
                if is_ctx and not need_ctx:
                    return None
                qbb, qb_r = nxt(qb); gbb, gb_r = nxt(gb)
                P.op("sp", lambda e, qbb=qbb, c=c, h=h: e.dma_start(
                    out=qbb, in_=self.QKVG[c * 128:(c + 1) * 128, h * 256:(h + 1) * 256]),
                    writes=[qb_r], dma=True)
                P.op("sp", lambda e, gbb=gbb, c=c, h=h: e.dma_start(
                    out=gbb, in_=self.QKVG[c * 128:(c + 1) * 128, 8192 + h * 512:8192 + (h + 1) * 512]),
                    writes=[gb_r], dma=True)
                q3b, q3_r = nxt(q3)
                P.op("act", lambda e, q3b=q3b, qbb=qbb: e.activation(out=q3b[:, 0, :], in_=qbb, func=AF.Copy),
                     reads=[qb_r], writes=[q3_r])
                P.op("dve", lambda e, q3b=q3b, qbb=qbb, h=h: e.tensor_scalar(
                    out=q3b[:, 1, :], in0=qbb, scalar1=dec[:, 0, h:h + 1], scalar2=None, op0=ALU.mult),
                    reads=[qb_r, r], writes=[q3_r])
                P.op("dve", lambda e, q3b=q3b, qbb=qbb, h=h: e.tensor_scalar(
                    out=q3b[:, 2, :], in0=qbb, scalar1=dec[:, 2, h:h + 1], scalar2=None, op0=ALU.mult),
                    reads=[qb_r, r], writes=[q3_r])
                qTb, qT_r = nxt(qT)
                self.transpose_to(qTb, qT_r, q3b.rearrange("p a b -> p (a b)"), q3_r, 6, "act")
                kTb, kT_r = nxt(kT)
                self.transpose_to(kTb, kT_r, kk[:, c, :], kk_r, 2, "dve")
                for dc in range(2):
                    P.op("pe", lambda e, dc=dc, kTb=kTb, qTb=qTb: e.matmul(
                        self.ps[4][:, 0:128], lhsT=kTb[:, dc * 128:(dc + 1) * 128], rhs=qTb[:, dc * 128:(dc + 1) * 128],
                        start=(dc == 0), stop=(dc == 1)), reads=[kT_r, qT_r], writes=[self.ps_r[4]])
                sTb, sT_r = nxt(sT)
                P.op("dve", lambda e, sTb=sTb, h=h: e.tensor_tensor(out=sTb, in0=self.ps[4][:, 0:128], in1=MT[:, h, :], op=ALU.mult),
                     reads=[self.ps_r[4], r], writes=[sT_r])
                return (qTb, qT_r, sTb, sT_r, gbb, gb_r)

            def stage_b(idx, c, cx):
                qTb, qT_r, sTb, sT_r, gbb, gb_r = cx
                use_f = idx > 0
                use_b = c != nlt + 1
                terms = [("s", 0)]
                if use_f:
                    terms += [("f", 0), ("f", 1)]
                if use_b:
                    terms += [("b", 0), ("b", 1)]
                pso, pso_r = self.ps[5 + (idx % 2)], self.ps_r[5 + (idx % 2)]
                sl = slot_of[c] if use_b else 0
                for ti_, (kind, dc) in enumerate(terms):
                    st_, sp_ = (ti_ == 0), (ti_ == len(terms) - 1)
                    if kind == "s":
                        P.op("pe", lambda e, sTb=sTb, c=c, pso=pso, st_=st_, sp_=sp_: e.matmul(
                            pso[:, :], lhsT=sTb, rhs=vv[:, c, :], start=st_, stop=sp_), reads=[sT_r, vv_r], writes=[pso_r])
                    elif kind == "f":
                        P.op("pe", lambda e, qTb=qTb, dc=dc, pso=pso, st_=st_, sp_=sp_: e.matmul(
                            pso[:, :], lhsT=qTb[:, 256 + dc * 128:256 + (dc + 1) * 128], rhs=Sf[:, dc, :], start=st_, stop=sp_),
                            reads=[qT_r, Sf_r], writes=[pso_r])
                    else:
                        P.op("pe", lambda e, qTb=qTb, dc=dc, pso=pso, sl=sl, st_=st_, sp_=sp_: e.matmul(
                            pso[:, :], lhsT=qTb[:, 512 + dc * 128:512 + (dc + 1) * 128], rhs=Sb[:, sl, dc, :], start=st_, stop=sp_),
                            reads=[qT_r, Sb_r[sl]], writes=[pso_r])
                stb, st_r = nxt(st)
                onb, on_r = nxt(onr)
                ogb, og_r = nxt(og)
                P.op("act", lambda e, onb=onb, pso=pso, stb=stb: e.activation(out=onb, in_=pso[:, :], func=AF.Copy, accum_out=stb[:, 0:1]),
                     reads=[pso_r], writes=[on_r, st_r])
                P.op("dve", lambda e, stb=stb: e.tensor_scalar(out=stb[:, 1:2], in0=stb[:, 0:1], scalar1=-1.0 / 512, scalar2=None, op0=ALU.mult),
                     reads=[st_r], writes=[st_r])
                P.op("act", lambda e, ogb=ogb, onb=onb, stb=stb: e.activation(out=ogb, in_=onb, func=AF.Square, bias=stb[:, 1:2], accum_out=stb[:, 2:3]),
                     reads=[on_r, st_r], writes=[og_r, st_r])
                P.op("act", lambda e, stb=stb: e.activation(out=stb[:, 3:4], in_=stb[:, 2:3], func=AF.Sqrt, scale=1.0 / 512, bias=self.epsc[:, 0:1]),
                     reads=[st_r, self.c_r], writes=[st_r])
                P.op("dve", lambda e, stb=stb: e.reciprocal(out=stb[:, 3:4], in_=stb[:, 3:4]), reads=[st_r], writes=[st_r])
                P.op("dve", lambda e, onb=onb, stb=stb: e.tensor_scalar(
                    out=onb, in0=onb, scalar1=stb[:, 1:2], scalar2=stb[:, 3:4], op0=ALU.add, op1=ALU.mult),
                    reads=[on_r, st_r], writes=[on_r])
                P.op("dve", lambda e, ogb=ogb, onb=onb, gbb=gbb: e.tensor_tensor(out=ogb, in0=onb, in1=gbb, op=ALU.mult),
                     reads=[on_r, gb_r], writes=[og_r])
                ogTb, ogT_r = nxt(ogT)
                self.transpose_to(ogTb, ogT_r, ogb, og_r, 4, "act")
                dst = self.OGT[h * 512:(h + 1) * 512, c * 128:(c + 1) * 128].rearrange("(k p) t -> p k t", p=128)
                P.op("sp", lambda e, dst=dst, ogTb=ogTb: e.dma_start(out=dst, in_=ogTb.rearrange("p (k t) -> p k t", t=128)),
                     reads=[ogT_r], writes=[Res()], dma=True)

            pend = stage_a(0, forder[0])
            for idx, c in enumerate(forder):
                nx_ = stage_a(idx + 1, forder[idx + 1]) if idx + 1 < len(forder) else None
                if pend is not None:
                    stage_b(idx, c, pend)
                if idx < len(forder) - 1:
                    state_update(c, h, 1, 0)
                    P.op("act", lambda e: e.activation(out=Sf, in_=Sm, func=AF.Copy), reads=[Sm_r], writes=[Sf_r])
                pend = nx_
        A.reset(m0)

    def na_core(self, l):
        P, A = self.P, self.A
        m0 = A.mark()
        self.tb_bank = 0
        j = l // 2
        nt, nlt = self.nt, self.nlt
        need_ctx = l != self.depth - 1
        scale = NADH ** -0.5
        ring = lambda shape, dt, n: [(A.alloc(shape, dt), Res()) for _ in range(n)]
        qk = ring([nt, 128], BF16, 2)
        vv = ring([nt, 128], BF16, 2)
        qTh = ring([nt, 128], BF16, 2)
        kTh = ring([nt, 128], BF16, 2)
        oTh = ring([nt * 128], BF16, 2)
        nab = ring([5, 640], F32, 2)
        ssb = ring([896], F32, 2)
        pb = ring([896], BF16, 2)
        pT = ring([896], BF16, 2)
        mx = ring([4], F32, 3)
        ob = ring([128], BF16, 2)
        it = {}

        def nxt(rg):
            k = it.get(id(rg), 0)
            it[id(rg)] = k + 1
            return rg[k % len(rg)]

        for h in range(NAH):
            qtm, qtm_r = nxt(qk)
            srcq = self.QKVG[:, h * 128:(h + 1) * 128].rearrange("(t p) f -> p t f", p=128)
            P.op("sp", lambda e, qtm=qtm, srcq=srcq: e.dma_start(out=qtm, in_=srcq), reads=[self.QKVG_r], writes=[qtm_r], dma=True)
            ktm, ktm_r = nxt(qk)
            srck = self.QKVG[:, D + h * 128:D + (h + 1) * 128].rearrange("(t p) f -> p t f", p=128)
            P.op("sp", lambda e, ktm=ktm, srck=srck: e.dma_start(out=ktm, in_=srck), reads=[self.QKVG_r], writes=[ktm_r], dma=True)
            vb, vb_r = nxt(vv)
            srcv = self.QKVG[:, 2 * D + h * 128:2 * D + (h + 1) * 128].rearrange("(t p) f -> p t f", p=128)
            P.op("sp", lambda e, vb=vb, srcv=srcv: e.dma_start(out=vb, in_=srcv), reads=[self.QKVG_r], writes=[vb_r], dma=True)
            nb, nb_r = nxt(nab)
            P.op("sp", lambda e, nb=nb, h=h: e.dma_start(out=nb, in_=self.nab_in[j * NAH + h].rearrange("p (a b) -> p a b", b=640)),
                 writes=[nb_r], dma=True)
            qT, qT_r = nxt(qTh)
            kT, kT_r = nxt(kTh)
            for t0 in range(0, nt, 8):
                n = min(8, nt - t0)
                self.transpose_to(qT[:, t0:t0 + n, :].rearrange("p a b -> p (a b)"), qT_r,
                                  qtm[:, t0:t0 + n, :].rearrange("p a b -> p (a b)"), qtm_r, n, "act")
                self.transpose_to(kT[:, t0:t0 + n, :].rearrange("p a b -> p (a b)"), kT_r,
                                  ktm[:, t0:t0 + n, :].rearrange("p a b -> p (a b)"), ktm_r, n, "dve")
            oT, oT_r = nxt(oTh)
            ntq = nt if need_ctx else nlt
            def stage_a(t):
                is_ctx = t >= nlt
                if is_ctx:
                    keyt = [nlt, nlt + 1]
                    nloc = 0
                    w = ty = 0
                else:
                    w = min(max(t - 2, 0), nlt - 5)
                    keyt = [w + i for i in range(5)] + [nlt, nlt + 1]
                    nloc = 640
                    ty = 0 if t == 0 else 1 if t == 1 else 3 if t == nlt - 2 else 4 if t == nlt - 1 else 2
                nk = len(keyt) * 128
                b0, b1 = (2, 3) if t % 2 == 0 else (4, 5)
                s_b, s_r = nxt(ssb)
                if not is_ctx:
                    P.op("pe", lambda e, t=t, w=w, b0=b0, qT=qT, kT=kT: e.matmul(
                        self.ps[b0][:, 0:512], lhsT=qT[:, t, :], rhs=kT[:, w:w + 4, :].rearrange("p a b -> p (a b)"), start=True, stop=True),
                        reads=[qT_r, kT_r], writes=[self.ps_r[b0]])
                    P.op("pe", lambda e, t=t, w=w, b1=b1, qT=qT, kT=kT: e.matmul(
                        self.ps[b1][:, 0:128], lhsT=qT[:, t, :], rhs=kT[:, w + 4, :], start=True, stop=True),
                        reads=[qT_r, kT_r], writes=[self.ps_r[b1]])
                P.op("pe", lambda e, t=t, b1=b1, qT=qT, kT=kT: e.matmul(
                    self.ps[b1][:, 128:384], lhsT=qT[:, t, :], rhs=kT[:, nlt:nlt + 2, :].rearrange("p a b -> p (a b)"), start=True, stop=True),
                    reads=[qT_r, kT_r], writes=[self.ps_r[b1]])
                if not is_ctx:
                    P.op("dve", lambda e, s_b=s_b, b0=b0, nb=nb, ty=ty: e.scalar_tensor_tensor(
                        out=s_b[:, 0:512], in0=self.ps[b0][:, 0:512], scalar=scale, in1=nb[:, ty, 0:512], op0=ALU.mult, op1=ALU.add),
                        reads=[self.ps_r[b0], nb_r], writes=[s_r])
                    P.op("dve", lambda e, s_b=s_b, b1=b1, nb=nb, ty=ty: e.scalar_tensor_tensor(
                        out=s_b[:, 512:640], in0=self.ps[b1][:, 0:128], scalar=scale, in1=nb[:, ty, 512:640], op0=ALU.mult, op1=ALU.add),
                        reads=[self.ps_r[b1], nb_r], writes=[s_r])
                P.op("act", lambda e, s_b=s_b, b1=b1, nloc=nloc: e.activation(
                    out=s_b[:, nloc:nloc + 256], in_=self.ps[b1][:, 128:384], func=AF.Copy, scale=scale),
                    reads=[self.ps_r[b1]], writes=[s_r])
                m_, m_r = nxt(mx)
                P.op("dve", lambda e, m_=m_, s_b=s_b, nk=nk: e.tensor_reduce(out=m_[:, 0:1], in_=s_b[:, 0:nk], axis=AX.X, op=ALU.max),
                     reads=[s_r], writes=[m_r])
                P.op("dve", lambda e, m_=m_: e.tensor_scalar(out=m_[:, 1:2], in0=m_[:, 0:1], scalar1=-1.0, scalar2=None, op0=ALU.mult),
                     reads=[m_r], writes=[m_r])
                p_, p_r = nxt(pb)
                P.op("act", lambda e, p_=p_, s_b=s_b, m_=m_, nk=nk: e.activation(
                    out=p_[:, 0:nk], in_=s_b[:, 0:nk], func=AF.Exp, bias=m_[:, 1:2], accum_out=m_[:, 2:3]),
                    reads=[s_r, m_r], writes=[p_r, m_r])
                P.op("dve", lambda e, m_=m_: e.reciprocal(out=m_[:, 3:4], in_=m_[:, 2:3]), reads=[m_r], writes=[m_r])
                return (t, keyt, nk, p_, p_r, m_, m_r)

            def stage_b(cx):
                t, keyt, nk, p_, p_r, m_, m_r = cx
                pT_, pT_r = nxt(pT)
                self.transpose_to(pT_[:, 0:nk], pT_r, p_[:, 0:nk], p_r, len(keyt), "act")
                pso, pso_r = self.ps[6 + t % 2], self.ps_r[6 + t % 2]
                for ki, kt in enumerate(keyt):
                    P.op("pe", lambda e, ki=ki, kt=kt, pT_=pT_, vb=vb, pso=pso: e.matmul(
                        pso[:, 0:128], lhsT=pT_[:, ki * 128:(ki + 1) * 128], rhs=vb[:, kt, :],
                        start=(ki == 0), stop=(ki == len(keyt) - 1)), reads=[pT_r, vb_r], writes=[pso_r])
                o_, o_r = nxt(ob)
                P.op("dve", lambda e, o_=o_, pso=pso, m_=m_: e.tensor_scalar(
                    out=o_, in0=pso[:, 0:128], scalar1=m_[:, 3:4], scalar2=None, op0=ALU.mult),
                    reads=[pso_r, m_r], writes=[o_r])
                self.transpose_to(oT[:, t * 128:(t + 1) * 128], oT_r, o_, o_r, 1, "act" if t % 2 else "dve")

            pend = stage_a(0)
            for t in range(ntq):
                nx_ = stage_a(t + 1) if t + 1 < ntq else None
                stage_b(pend)
                pend = nx_
            ntk = ntq * 128
            P.op("sp", lambda e, oT=oT, h=h, ntk=ntk: e.dma_start(out=self.OGT[h * 128:(h + 1) * 128, 0:ntk], in_=oT[:, 0:ntk]),
                 reads=[oT_r], writes=[Res()], dma=True)
        A.reset(m0)


def _col(v):
    v = np.asarray(v, np.float32)
    lead = v.shape[:-1]
    return np.ascontiguousarray(np.moveaxis(v.reshape(lead + (16, 128)), -1, 0))


def _rope_tables(nlat):
    t = np.arange(nlat)
    row = (t // GRID_W).astype(np.float32)
    col = (t % GRID_W).astype(np.float32)
    half = RDK // 2
    freqs = (10000.0 ** (-np.arange(0, half, 2, dtype=np.float32) / half)).astype(np.float32)
    ar = row[:, None] * freqs
    ac = col[:, None] * freqs
    cos = np.concatenate([np.cos(ar), np.cos(ar), np.cos(ac), np.cos(ac)], -1)
    sin = np.concatenate([-np.sin(ar), np.sin(ar), -np.sin(ac), np.sin(ac)], -1)
    tab = np.concatenate([cos, sin], -1).astype(np.float32)
    return np.ascontiguousarray(tab.reshape(nlat // 128, 128, 512))


def _na_bias(rpb, rows):
    nlt = rows // 2
    nn, H = rpb.shape[0], rpb.shape[1]
    out = np.full((nn * H, 128, 5, 640), -30000.0, np.float32)
    types = [0, 1, 2, nlt - 2, nlt - 1]
    qa = np.arange(128) // 64
    qc = np.arange(128) % 64
    ku = np.arange(640) // 64
    kc = np.arange(640) % 64
    cs = np.clip(qc - NA_KC // 2, 0, GRID_W - NA_KC)
    for ti, t in enumerate(types):
        w = min(max(t - 2, 0), nlt - 5)
        r = 2 * t + qa
        rs = np.clip(r - NA_KR // 2, 0, rows - NA_KR)
        krow = 2 * w + ku
        okr = (krow[None, :] >= rs[:, None]) & (krow[None, :] < rs[:, None] + NA_KR)
        okc = (kc[None, :] >= cs[:, None]) & (kc[None, :] < cs[:, None] + NA_KC)
        ok = okr & okc
        dr = np.clip(krow[None, :] - r[:, None] + (NA_KR - 1), 0, 2 * NA_KR - 2)
        dc = np.clip(kc[None, :] - qc[:, None] + (NA_KC - 1), 0, 2 * NA_KC - 2)
        for n in range(nn):
            g = rpb[n][:, dr, dc]
            out[n * H:(n + 1) * H, :, ti, :] = np.where(ok[None], g, np.float32(-30000.0))
    return np.ascontiguousarray(out.reshape(nn * H, 128, 5 * 640))


def _consts():
    p = np.arange(128, dtype=np.float32)
    pos = np.stack([p + 1, 127 - p, p, 128 - p], 1)
    jj, ii = np.meshgrid(p, p, indexing="ij")
    D1 = np.maximum(ii - jj, 0)
    D2 = np.maximum(jj - ii, 0)
    return np.ascontiguousarray(np.concatenate([pos, D1, D2, np.eye(128, dtype=np.float32)], 1).astype(np.float32))


def make_in_maps(kb, inputs, batches):
    dep = kb.depth
    x, c, ctx, c_ctx = inputs["x"], inputs["c"], inputs["ctx"], inputs["c_ctx"]
    shared = {
        "adab": _col(np.asarray(inputs["ada_b"], np.float32).reshape(dep, 9, D)).reshape(128, dep * 144),
        "ng": _col(inputs["norm_g"]).reshape(128, dep * 48),
        "fg": _col(inputs["final_g"]).reshape(128, 16),
        "lgd": np.ascontiguousarray(np.broadcast_to(np.asarray(inputs["ret_log_decay"], np.float32).reshape(1, -1), (128, kb.nret * 16))),
        "cst": _consts(),
        "rope": _rope_tables(kb.nlat),
        "ada_w": np.asarray(inputs["ada_w"], np.float32),
        "ffn_w_gu": np.asarray(inputs["ffn_w_gu"], np.float32),
        "ffn_w_down": np.asarray(inputs["ffn_w_down"], np.float32),
        "ret_w_in": np.asarray(inputs["ret_w_in"], np.float32),
        "ret_w_out": np.asarray(inputs["ret_w_out"], np.float32),
        "na_w_in": np.asarray(inputs["na_w_in"], np.float32),
        "na_w_out": np.asarray(inputs["na_w_out"], np.float32),
    }
    lg = np.asarray(inputs["ret_log_decay"], np.float32).reshape(kb.nret, 16)
    lgp = np.zeros((kb.nret, 32), np.float32)
    lgp[:, :16] = lg
    shared["lgd"] = np.ascontiguousarray(np.broadcast_to(lgp.reshape(1, -1), (128, kb.nret * 32)))
    if kb.nna > 0:
        shared["nab"] = _na_bias(np.asarray(inputs["na_rpb"], np.float32), kb.rows)
    else:
        shared["nab"] = np.zeros((NAH, 128, 5 * 640), np.float32)
    maps = []
    for b in batches:
        m = dict(shared)
        m["xin"] = np.ascontiguousarray(np.concatenate([np.asarray(x[b], np.float32).T, np.asarray(ctx[b], np.float32).T], 1))
        m["cc"] = np.ascontiguousarray(np.concatenate([_col(c[b]), _col(c_ctx)], 1))
        maps.append(m)
    return maps


_CACHE = {}


def kernel(**inputs):
    x = np.asarray(inputs["x"])
    B, nlat = x.shape[0], x.shape[1]
    dep = np.asarray(inputs["ada_w"]).shape[0]
    key = (nlat, dep)
    if key not in _CACHE:
        kb = K(nlat, dep)
        _CACHE[key] = (kb, kb.build())
    kb, nc = _CACHE[key]
    n_cores = B
    batches = list(range(B))
    maps = make_in_maps(kb, inputs, batches)
    res = run_bass_kernel_spmd(nc, maps, core_ids=list(range(n_cores)))
    out = np.stack([np.ascontiguousarray(res.results[b]["out"].T) for b in range(B)], 0)
    return out.astype(np.float32)
```
